# Optimizing a Trainium2 kernel written in Bass

```python
import math
import jax, jax.numpy as jnp
from jax import lax
import numpy as np

D_MODEL = 1024
BATCH = 2
SEQ = 8192
DEPTH = 1

SG_CHUNK = 128
SG_WIDTH = D_MODEL
SG_GROUPS = 8
SG_GROUP_DIM = SG_WIDTH // SG_GROUPS
GLA_HEADS = 4
GLA_KEY_DIM = D_MODEL // 2
GLA_VAL_DIM = D_MODEL
GLA_HEAD_K = GLA_KEY_DIM // GLA_HEADS
GLA_HEAD_V = GLA_VAL_DIM // GLA_HEADS
GLA_GATE_RANK = 16
GLA_GATE_TEMP = 16.0
GLA_CHUNK = 64
D_FF = 4 * D_MODEL
LN_EPS = 1e-5
DEEPNORM_ALPHA = (2.0 * DEPTH) ** 0.25
DEEPNORM_BETA = (8.0 * DEPTH) ** -0.25
SPLITS = (SG_WIDTH, SG_WIDTH, GLA_KEY_DIM, GLA_KEY_DIM, GLA_VAL_DIM, GLA_VAL_DIM,
          GLA_GATE_RANK, D_MODEL, D_MODEL)
D_IN = sum(SPLITS)
SPLIT_POINTS = [int(s) for s in np.cumsum(SPLITS)[:-1]]

kernel_name = 'hybrid_sgu_gla_deepnorm_block'


def layer_norm(x, g, b):
    xf = x.astype(jnp.float32)
    mu = jnp.mean(xf, axis=-1, keepdims=True)
    var = jnp.mean(jnp.square(xf - mu), axis=-1, keepdims=True)
    y = (xf - mu) * lax.rsqrt(var + LN_EPS) * g.astype(jnp.float32) + b.astype(jnp.float32)
    return y.astype(x.dtype)


def spatial_gating(u, v, ln_g, ln_b, w_s, b_s):
    bsz, t, _ = v.shape
    n = t // SG_CHUNK
    u = jax.nn.gelu(u)
    v = layer_norm(jax.nn.gelu(v), ln_g, ln_b)
    vc = v.reshape(bsz, n, SG_CHUNK, SG_GROUPS, SG_GROUP_DIM)
    causal = jnp.tril(jnp.ones((SG_CHUNK, SG_CHUNK), dtype=bool))
    ws = jnp.where(causal[None], w_s, jnp.zeros_like(w_s))
    mixed = jnp.einsum('gts,bnsgd->bntgd', ws, vc) + jnp.transpose(b_s)[:, :, None]
    return u * mixed.reshape(bsz, t, SG_WIDTH)


def gla_chunked(q, k, v, log_f):
    bsz, t, h, dk = q.shape
    dv = v.shape[-1]
    n = t // GLA_CHUNK

    def to_chunks(a):
        return a.astype(jnp.float32).reshape(bsz, n, GLA_CHUNK, h, a.shape[-1]).transpose(1, 0, 3, 2, 4)

    qc, kc, vc, gc = to_chunks(q * (GLA_HEAD_K ** -0.5)), to_chunks(k), to_chunks(v), to_chunks(log_f)
    causal = jnp.tril(jnp.ones((GLA_CHUNK, GLA_CHUNK), dtype=bool))[:, :, None]

    def step(state, inp):
        qb, kb, vb, gb = inp
        cum = jnp.cumsum(gb, axis=-2)
        diff = cum[..., :, None, :] - cum[..., None, :, :]
        decay = jnp.exp(jnp.where(causal, diff, -jnp.inf))
        scores = jnp.einsum('bhtd,bhsd,bhtsd->bhts', qb, kb, decay)
        o_intra = jnp.einsum('bhts,bhsv->bhtv', scores, vb)
        o_inter = jnp.einsum('bhtd,bhdv->bhtv', qb * jnp.exp(cum), state)
        total = cum[..., -1, :]
        k_dec = kb * jnp.exp(total[..., None, :] - cum)
        new_state = jnp.exp(total)[..., None] * state + jnp.einsum('bhsd,bhsv->bhdv', k_dec, vb)
        return new_state, o_intra + o_inter

    s0 = jnp.zeros((bsz, h, dk, dv), dtype=jnp.float32)
    _, out = lax.scan(step, s0, (qc, kc, vc, gc))
    return out.transpose(1, 0, 3, 2, 4).reshape(bsz, t, h, dv)


def setup_inputs(seed: int = 0) -> dict:
    key = jax.random.key(seed)
    ks = jax.random.split(key, 20)
    f32 = jnp.float32
    x = jax.random.normal(ks[0], (BATCH, SEQ, D_MODEL), f32)
    w_in = jax.random.normal(ks[1], (DEPTH, D_MODEL, D_IN), f32) * D_MODEL ** -0.5
    col_scale = np.ones((D_IN,), np.float32)
    col_scale[:2 * SG_WIDTH] = DEEPNORM_BETA
    v0 = 2 * SG_WIDTH + 2 * GLA_KEY_DIM
    col_scale[v0:v0 + GLA_VAL_DIM] = DEEPNORM_BETA
    w_in = w_in * jnp.asarray(col_scale)
    b_in = 0.02 * jax.random.normal(ks[2], (DEPTH, D_IN), f32)
    sg_ln_g = 1.0 + 0.05 * jax.random.normal(ks[3], (DEPTH, SG_WIDTH), f32)
    sg_ln_b = 0.02 * jax.random.normal(ks[4], (DEPTH, SG_WIDTH), f32)
    sg_w_s = 0.5 * jax.random.normal(ks[5], (DEPTH, SG_GROUPS, SG_CHUNK, SG_CHUNK), f32) * SG_CHUNK ** -0.5
    sg_b_s = 1.0 + 0.1 * jax.random.normal(ks[6], (DEPTH, SG_GROUPS, SG_CHUNK), f32)
    gla_w_gate2 = jax.random.normal(ks[7], (DEPTH, GLA_GATE_RANK, GLA_KEY_DIM), f32) * GLA_GATE_RANK ** -0.5
    gla_b_gate = 0.1 * jax.random.normal(ks[8], (DEPTH, GLA_KEY_DIM), f32)
    gla_norm_g = 1.0 + 0.05 * jax.random.normal(ks[9], (DEPTH, GLA_VAL_DIM), f32)
    w_out = jax.random.normal(ks[10], (DEPTH, D_MODEL, D_MODEL), f32) * (D_MODEL ** -0.5 * DEEPNORM_BETA)
    ln1_g = 1.0 + 0.05 * jax.random.normal(ks[11], (DEPTH, D_MODEL), f32)
    ln1_b = 0.02 * jax.random.normal(ks[12], (DEPTH, D_MODEL), f32)
    w_ff1 = jax.random.normal(ks[13], (DEPTH, D_MODEL, D_FF), f32) * (D_MODEL ** -0.5 * DEEPNORM_BETA)
    w_ff2 = jax.random.normal(ks[14], (DEPTH, D_FF, D_MODEL), f32) * (D_FF ** -0.5 * DEEPNORM_BETA)
    ln2_g = 1.0 + 0.05 * jax.random.normal(ks[15], (DEPTH, D_MODEL), f32)
    ln2_b = 0.02 * jax.random.normal(ks[16], (DEPTH, D_MODEL), f32)
    return {'x': x, 'w_in': w_in, 'b_in': b_in, 'sg_ln_g': sg_ln_g, 'sg_ln_b': sg_ln_b,
            'sg_w_s': sg_w_s, 'sg_b_s': sg_b_s, 'gla_w_gate2': gla_w_gate2, 'gla_b_gate': gla_b_gate,
            'gla_norm_g': gla_norm_g, 'w_out': w_out, 'ln1_g': ln1_g, 'ln1_b': ln1_b,
            'w_ff1': w_ff1, 'w_ff2': w_ff2, 'ln2_g': ln2_g, 'ln2_b': ln2_b}


def reference(x, w_in, b_in, sg_ln_g, sg_ln_b, sg_w_s, sg_b_s, gla_w_gate2, gla_b_gate,
              gla_norm_g, w_out, ln1_g, ln1_b, w_ff1, w_ff2, ln2_g, ln2_b):
    bsz, t, _ = x.shape
    h = x
    for l in range(DEPTH):
        p = jnp.einsum('btd,de->bte', h, w_in[l]) + b_in[l]
        a_u, a_v, q, k, v, r, a_low, g_a, g_b = jnp.split(p, SPLIT_POINTS, axis=-1)
        y_a = spatial_gating(a_u, a_v, sg_ln_g[l], sg_ln_b[l], sg_w_s[l], sg_b_s[l])
        log_f = jax.nn.log_sigmoid((a_low @ gla_w_gate2[l] + gla_b_gate[l]).astype(jnp.float32)) / GLA_GATE_TEMP
        o = gla_chunked(q.reshape(bsz, t, GLA_HEADS, GLA_HEAD_K),
                        k.reshape(bsz, t, GLA_HEADS, GLA_HEAD_K),
                        v.reshape(bsz, t, GLA_HEADS, GLA_HEAD_V),
                        log_f.reshape(bsz, t, GLA_HEADS, GLA_HEAD_K))
        o = o * lax.rsqrt(jnp.mean(jnp.square(o), axis=-1, keepdims=True) + LN_EPS)
        y_b = (o.reshape(bsz, t, GLA_VAL_DIM) * gla_norm_g[l].astype(jnp.float32)).astype(h.dtype) * jax.nn.silu(r)
        m = jax.nn.sigmoid(g_a) * y_a + jax.nn.sigmoid(g_b) * y_b
        h = layer_norm(DEEPNORM_ALPHA * h + m @ w_out[l], ln1_g[l], ln1_b[l])
        f = jnp.square(jax.nn.relu(h @ w_ff1[l])) @ w_ff2[l]
        h = layer_norm(DEEPNORM_ALPHA * h + f, ln2_g[l], ln2_b[l])
    return h
```

```python
import math
from contextlib import ExitStack

import numpy as np
import concourse.bass as bass
import concourse.mybir as mybir
from concourse.bass_utils import run_bass_kernel_spmd

F32 = mybir.dt.float32
BF16 = mybir.dt.bfloat16
AF = mybir.ActivationFunctionType
ALU = mybir.AluOpType

ENGS = ["pe", "act", "dve", "pool", "sp"]
N_DMA_SEMS = 24

D = 1024
NTOK = 2048
NT = NTOK // 128
ST = 256
NST = NTOK // ST
TPS = ST // 128
D_IN = 7184
C_U, C_VS, C_Q, C_K, C_V, C_R, C_AL, C_GA, C_GB = 0, 1024, 2048, 2560, 3072, 4096, 5120, 5136, 6160
D_FF = 4096
LN_EPS = 1e-5
ALPHA = 2.0 ** 0.25
GATE_TEMP = 16.0
Q_SCALE = 128.0 ** -0.5


def bcol(c0):
    return c0 // 128 if c0 < C_AL else 40 + (c0 - C_GA) // 128


COL_SGG, COL_SGB, COL_GN = 56, 64, 72


class Buf:
    def __init__(self, name, excl=False):
        self.name = name
        self.excl = excl
        self.w = None
        self.w_read = False
        self.r = {}


class Prog:
    def __init__(self):
        self.q = {e: [] for e in ENGS}
        self.seen = {e: {} for e in ENGS}
        self.dma_cnt = [0] * (N_DMA_SEMS + 1)
        self.dma_rr = 0
        self.dma_rr_pool = 0
        self.last_op = {e: None for e in ENGS}

    def _wait(self, eng, tok):
        if tok is None:
            return
        if tok[0] == "eng":
            _, e2, idx = tok
            key = ("eng", e2)
            if self.seen[eng].get(key, -1) >= idx:
                return
            self.seen[eng][key] = idx
            self.q[e2][idx]["inc"] = True
            self.q[eng].append(dict(kind="wait", tok=tok))
        else:
            _, k, val = tok
            key = ("dma", k)
            if self.seen[eng].get(key, -1) >= val:
                return
            self.seen[eng][key] = val
            self.q[eng].append(dict(kind="wait", tok=tok))

    def _deps(self, eng, reads, writes, is_dma):
        toks = []
        writes = list(writes)
        excl_reads = []
        for b in reads:
            if b.excl:
                if b not in writes and b not in excl_reads:
                    excl_reads.append(b)
                continue
            if b.w is not None:
                toks.append(b.w)
        for b in writes:
            if b.w is not None:
                toks.append(b.w)
            toks.extend(b.r.values())
        for b in excl_reads:
            t = b.w
            if t is None:
                continue
            if t[0] == "eng" and t[1] == eng and not is_dma and b.w_read:
                continue
            toks.append(t)
        out = []
        for t in toks:
            if t[0] == "eng" and t[1] == eng and not is_dma and eng == "pe":
                continue
            out.append(t)
        return out, writes, excl_reads

    def op(self, eng, fn, reads=(), writes=()):
        reads = [b for b in reads if b is not None]
        writes = [b for b in writes if b is not None]
        toks, writes2, excl_reads = self._deps(eng, reads, writes, False)
        for t in toks:
            self._wait(eng, t)
        idx = len(self.q[eng])
        self.q[eng].append(dict(kind="op", fn=fn, inc=False))
        tok = ("eng", eng, idx)
        self.last_op[eng] = tok
        for b in writes2:
            b.w = tok
            b.w_read = False
            b.r = {}
        for b in excl_reads:
            b.w = tok
            b.w_read = True
            b.r = {}
        for b in reads:
            if b not in writes2 and b not in excl_reads:
                b.r[eng] = tok
        return tok

    def dma(self, eng, fn, reads=(), writes=()):
        reads = [b for b in reads if b is not None]
        writes = [b for b in writes if b is not None]
        toks, writes2, _er = self._deps(eng, reads, writes, True)
        assert not _er
        for t in toks:
            self._wait(eng, t)
        half = N_DMA_SEMS // 2
        if eng == "pool":
            k = half + self.dma_rr_pool
            self.dma_rr_pool = (self.dma_rr_pool + 1) % half
        else:
            k = self.dma_rr
            self.dma_rr = (self.dma_rr + 1) % half
        prev = self.dma_cnt[k]
        if prev > 0:
            self._wait(eng, ("dma", k, prev))
        self.dma_cnt[k] = prev + 16
        tok = ("dma", k, prev + 16)
        self.q[eng].append(dict(kind="dma", fn=fn, k=k))
        for b in writes2:
            b.w = tok
            b.r = {}
        for b in reads:
            if b not in writes2:
                b.r[("dma", k)] = tok
        return tok

    def ext(self, eng, fn, reads=(), writes=()):
        toks, writes2, _er = self._deps(eng, list(reads), list(writes), True)
        for t in toks:
            self._wait(eng, t)
        k = N_DMA_SEMS
        self.dma_cnt[k] += 1
        tok = ("dma", k, self.dma_cnt[k])
        self.q[eng].append(dict(kind="ext", fn=fn, k=k))
        for b in writes2:
            b.w = tok
            b.r = {}
        for b in reads:
            if b not in writes2:
                b.r[("dma", k)] = tok
        return tok

    def fence_tokens(self):
        toks = {}
        for e in ENGS:
            if self.last_op[e] is not None:
                toks[("f", e)] = self.last_op[e]
        for k in range(N_DMA_SEMS // 2):
            if self.dma_cnt[k]:
                toks[("fd", k)] = ("dma", k, self.dma_cnt[k])
        return toks

    def fence(self, bufs):
        toks = self.fence_tokens()
        for b in bufs:
            b.w = None
            b.w_read = False
            b.r = dict(toks)

    def wait_all(self, eng, bufs):
        for b in bufs:
            if b.w is not None:
                self._wait(eng, b.w)

    def emit(self, block, esem, dsems):
        cnt_at = {}
        for e in ENGS:
            c = 0
            for i, r in enumerate(self.q[e]):
                if r["kind"] == "op" and r["inc"]:
                    c += 1
                    cnt_at[(e, i)] = c
        qs = self.q

        def run(e, engine):
            for r in qs[e]:
                if r["kind"] == "wait":
                    t = r["tok"]
                    if t[0] == "eng":
                        engine.wait_ge(esem[t[1]], cnt_at[(t[1], t[2])])
                    else:
                        engine.wait_ge(dsems[t[1]], t[2])
                elif r["kind"] == "op":
                    inst = r["fn"](engine)
                    if r["inc"]:
                        inst.then_inc(esem[e], 1)
                elif r["kind"] == "ext":
                    inst = r["fn"](engine)
                    inst.then_inc(dsems[r["k"]], 1)
                else:
                    inst = r["fn"](engine)
                    inst.then_inc(dsems[r["k"]], 16)

        @block.tensor
        def _(eng):
            run("pe", eng)

        @block.scalar
        def _(eng):
            run("act", eng)

        @block.vector
        def _(eng):
            run("dve", eng)

        @block.gpsimd
        def _(eng):
            run("pool", eng)

        @block.sync
        def _(eng):
            run("sp", eng)


class T:
    def __init__(self, ap, name, buf=None):
        self.ap = ap
        self.b = buf if buf is not None else Buf(name)

    def __getitem__(self, k):
        return self.ap[k]


SB_LIMIT = 212480


def build_nc(debug=False):
    nc = bass.Bass("TRN2", target_bir_lowering=False)

    def din(name, shape):
        return nc.dram_tensor(name, shape, F32, kind="ExternalInput").ap()

    x = din("x", [NTOK, D])
    w_in = din("w_in", [D, D_IN])
    b_in = din("b_in", [1, D_IN])
    sg_ln_g = din("sg_ln_g", [1, D])
    sg_ln_b = din("sg_ln_b", [1, D])
    sg_w_s = din("sg_w_s", [8 * 128, 128])
    sg_b_s = din("sg_b_s", [1, D])
    wg2_d = din("gla_w_gate2", [16, 512])
    bgate_d = din("gla_b_gate", [1, 512])
    gnorm_d = din("gla_norm_g", [1, D])
    w_out = din("w_out", [D, D])
    ln1_g = din("ln1_g", [1, D])
    ln1_b = din("ln1_b", [1, D])
    w_ff1 = din("w_ff1", [D, D_FF])
    w_ff2 = din("w_ff2", [D_FF, D])
    ln2_g = din("ln2_g", [1, D])
    ln2_b = din("ln2_b", [1, D])
    cmask_d = din("cmask", [128, 8])
    out = nc.dram_tensor("out", [NTOK, D], F32, kind="ExternalOutput").ap()
    dbg = None
    if debug:
        dbg = nc.dram_tensor("dbg", [128, 8 * NTOK], F32, kind="ExternalOutput").ap()
    cc_in = nc.dram_tensor("cc_in", [128, 1028], F32)
    cc_out = nc.dram_tensor("cc_out", [512, 1028], F32)

    P = Prog()
    with ExitStack() as es:
        R = es.enter_context(nc.sbuf_tensor("R", [128, SB_LIMIT // 4], F32))
        banks = [es.enter_context(nc.psum_tensor("bank%d" % i, [128, 512], F32)) for i in range(8)]
        bankb = [Buf("bank%d" % i, excl=True) for i in range(8)]
        esem = {e: es.enter_context(nc.semaphore("s_" + e)) for e in ["pe", "act", "dve", "pool"]}
        dsems = [es.enter_context(nc.semaphore("d%d" % i)) for i in range(N_DMA_SEMS + 1)]
        block = es.enter_context(nc.Block())

        def carve(off, nbytes, dtype, name, shape=None, parts=128, buf=None):
            assert off % 4 == 0 and nbytes % 4 == 0 and off + nbytes <= SB_LIMIT, (name, off, nbytes)
            ap = R[0:parts, off // 4:(off + nbytes) // 4]
            if dtype is BF16:
                ap = ap.bitcast(BF16)
            if shape is not None:
                assert len(shape) == 2
                ap = ap.rearrange("p (a b) -> p a b", a=shape[0])
            return T(ap, name, buf)

        class Bump:
            def __init__(self, start, end):
                self.start, self.end, self.cur = start, end, start

            def alloc(self, nbytes, dtype, name, shape=None, parts=128):
                t = carve(self.cur, nbytes, dtype, name, shape, parts)
                self.cur += (nbytes + 31) // 32 * 32
                assert self.cur <= self.end, (name, self.cur, self.end)
                return t

        OFF_MT = 0
        OFF_CONST = 32768
        OFF_R1 = 36864
        OFF_W = OFF_R1 + 147456
        mT = carve(OFF_MT, 32768, BF16, "mT", (8, NTOK))
        mTb = [Buf("mT%d" % i) for i in range(NT)]
        cst = Bump(OFF_CONST, OFF_R1)
        ident = cst.alloc(512, F32, "ident")
        cols = cst.alloc(80 * 4, F32, "cols")
        c_one = cst.alloc(4, F32, "c_one")
        c_lns = cst.alloc(4, F32, "c_lns")
        c_mh = cst.alloc(16, F32, "c_mh")
        c_zero = cst.alloc(4, F32, "c_zero")
        c_eps = cst.alloc(4, F32, "c_eps")
        cmask = cst.alloc(32, F32, "cmask")
        balow = cst.alloc(4, F32, "balow")
        st6 = cst.alloc(48, F32, "st6")
        st6_hb = [Buf("st6_h%d" % h) for h in range(2)]
        mv = cst.alloc(8, F32, "mv")
        rstd = cst.alloc(16, F32, "rstd")
        ssq = cst.alloc(16, F32, "ssq")
        dec = cst.alloc(16, F32, "dec")
        Dacc = cst.alloc(16, F32, "Dacc")
        dprime = cst.alloc(16, F32, "dprime")
        wal = cst.alloc(8 * 16 * 2, BF16, "wal", (8, 16))

        xT = carve(OFF_R1, 32768, BF16, "xT", (8, NTOK))
        xTb = [Buf("xT%d" % i) for i in range(NT)]
        arena = carve(OFF_R1 + 32768, 65536, BF16, "arena", (8, 4096))
        arb = {(kc, s): Buf("ar%d_%d" % (kc, s)) for kc in range(8) for s in range(4)}
        alT = carve(OFF_R1 + 98304, 8192, F32, "alT", parts=17)
        alTb = [Buf("alT%d" % n) for n in range(4)]
        p1 = Bump(OFF_R1 + 106496, SB_LIMIT)

        def pe_mm(out_ap, bank_i, lhsT, rhs, reads, start=True, stop=True):
            P.op("pe", lambda e: e.matmul(out_ap, lhsT, rhs, start=start, stop=stop), reads, [bankb[bank_i]])

        def pe_tr(out_ap, bank_i, in_ap, reads):
            P.op("pe", lambda e: e.transpose(out_ap, in_ap, ident.ap), list(reads) + [ident.b], [bankb[bank_i]])

        def act(out_ap, in_ap, func, reads, writes, bias=None, scale=1.0, accum=None):
            kw = {}
            if bias is not None:
                kw["bias"] = bias
            if accum is not None:
                kw["accum_out"] = accum
            P.op("act", lambda e: e.activation(out=out_ap, in_=in_ap, func=func, scale=scale, **kw), reads, writes)

        def tt(eng, out_ap, in0, in1, op, reads, writes):
            P.op(eng, lambda e: e.tensor_tensor(out=out_ap, in0=in0, in1=in1, op=op), reads, writes)

        def ts(eng, out_ap, in0, s1, s2, op0, op1, reads, writes):
            if op1 is None:
                P.op(eng, lambda e: e.tensor_scalar(out=out_ap, in0=in0, scalar1=s1, scalar2=None, op0=op0), reads, writes)
            else:
                P.op(eng, lambda e: e.tensor_scalar(out=out_ap, in0=in0, scalar1=s1, scalar2=s2, op0=op0, op1=op1), reads, writes)

        def stt(out_ap, in0, scalar, in1, op0, op1, reads, writes):
            P.op("dve", lambda e: e.scalar_tensor_tensor(out=out_ap, in0=in0, scalar=scalar, in1=in1, op0=op0, op1=op1), reads, writes)

        def cp(eng, out_ap, in_ap, reads, writes):
            if eng == "act":
                P.op("act", lambda e: e.copy(out=out_ap, in_=in_ap), reads, writes)
            else:
                P.op(eng, lambda e: e.tensor_copy(out=out_ap, in_=in_ap), reads, writes)

        def dma(eng, out_ap, in_ap, reads, writes):
            P.dma(eng, lambda e: e.dma_start(out=out_ap, in_=in_ap), reads, writes)

        def memset(eng, t, val):
            P.op(eng, lambda e: e.memset(t.ap, val), [], [t.b])

        def bk(i, a=0, b=512):
            return banks[i][:, a:b]

        def bk3(i, nb=4):
            return banks[i][:, :].rearrange("p (a b) -> p a b", a=nb)

        xs = [carve(SB_LIMIT - 8192 + i * 4096, 4096, F32, "xs%d" % i) for i in range(2)]
        for i in range(2):
            dma("sp", xs[i].ap, x[i * 128:(i + 1) * 128, :], [], [xs[i].b])
        memset("pool", c_one, 1.0)
        memset("pool", c_lns, math.log(Q_SCALE))
        memset("pool", c_mh, -0.5)
        memset("pool", c_zero, 0.0)
        memset("pool", c_eps, LN_EPS)
        memset("pool", ident, 1.0)
        P.op("pool", lambda e: e.affine_select(out=ident.ap, in_=ident.ap, pattern=[[-1, 128]], compare_op=ALU.is_equal,
                                               fill=0.0, base=0, channel_multiplier=1), [ident.b], [ident.b])
        dma("sp", cmask.ap, cmask_d[:, :], [], [cmask.b])
        dma("sp", balow.ap[0:16, :], b_in[0, C_AL:C_AL + 16].rearrange("(p o) -> p o", o=1), [], [balow.b])
        dma("pool", wal.ap, w_in[:, C_AL:C_AL + 16].rearrange("(k p) c -> p k c", p=128), [], [wal.b])

        rows = p1.alloc(512, F32, "rows")
        memset("pool", rows, 0.0)
        dma("sp", rows.ap[0:40, :], b_in[0, 0:C_AL].rearrange("(j p) -> j p", p=128), [], [rows.b])
        dma("sp", rows.ap[40:56, :], b_in[0, C_GA:D_IN].rearrange("(j p) -> j p", p=128), [], [rows.b])
        dma("sp", rows.ap[56:64, :], sg_ln_g[0, :].rearrange("(j p) -> j p", p=128), [], [rows.b])
        dma("sp", rows.ap[64:72, :], sg_ln_b[0, :].rearrange("(j p) -> j p", p=128), [], [rows.b])
        dma("sp", rows.ap[72:80, :], gnorm_d[0, :].rearrange("(j p) -> j p", p=128), [], [rows.b])
        P.wait_all("pe", [rows.b])
        for k in range(N_DMA_SEMS):
            if P.dma_cnt[k]:
                P._wait("pe", ("dma", k, P.dma_cnt[k]))
        pe_tr(bk(0, 0, 128), 0, rows.ap, [rows.b])
        cp("dve", cols.ap, bk(0, 0, 80), [bankb[0]], [cols.b])

        def col(j):
            return cols.ap[:, j:j + 1]

        def load_arena(dst, dstb, src, c0, ncols, a0):
            nkc = dst.ap.shape[1]
            for kc in range(nkc):
                o = 0
                while o < ncols:
                    a = a0 + o
                    n = min(ncols - o, 1024 - (a % 1024))
                    dma("pool", dst.ap[:, kc, a:a + n], src[kc * 128:(kc + 1) * 128, c0 + o:c0 + o + n],
                        [], [dstb[(kc, a // 1024)]])
                    o += n

        def arr(kc, a):
            return arb[(kc, a // 1024)]

        Linc = p1.alloc(512, F32, "Linc")
        Ust = p1.alloc(512, F32, "Ust")
        mask4 = p1.alloc(2048, F32, "mask4")
        wg2 = p1.alloc(2048, F32, "wg2", parts=17)
        bgate = p1.alloc(2048, F32, "bgate")
        bias_bc = p1.alloc(4096, F32, "bias_bc")
        S = p1.alloc(4096, F32, "S", (4, 256))
        S_hb = [Buf("S_h%d" % h) for h in range(4)]
        S_bf = p1.alloc(2048, BF16, "S_bf", (4, 256))
        maskT = p1.alloc(512, F32, "maskT")
        wst8 = p1.alloc(4096, F32, "wst8", (8, 128))
        p1_base = p1.cur

        memset("pool", Linc, -1.0 / GATE_TEMP)
        P.op("pool", lambda e: e.affine_select(out=Linc.ap, in_=Linc.ap, pattern=[[1, 128]], compare_op=ALU.is_ge,
                                               fill=0.0, base=0, channel_multiplier=-1), [Linc.b], [Linc.b])
        memset("pool", Ust, -1.0 / GATE_TEMP)
        P.op("pool", lambda e: e.affine_select(out=Ust.ap, in_=Ust.ap, pattern=[[-1, 128]], compare_op=ALU.is_gt,
                                               fill=0.0, base=0, channel_multiplier=1), [Ust.b], [Ust.b])
        memset("pool", mask4, 1.0)
        P.op("pool", lambda e: e.affine_select(out=mask4.ap, in_=mask4.ap, pattern=[[0, 4], [1, 128]], compare_op=ALU.is_ge,
                                               fill=0.0, base=0, channel_multiplier=-1), [mask4.b], [mask4.b])
        memset("pool", maskT, 1.0)
        P.op("pool", lambda e: e.affine_select(out=maskT.ap, in_=maskT.ap, pattern=[[-1, 128]], compare_op=ALU.is_ge,
                                               fill=0.0, base=0, channel_multiplier=1), [maskT.b], [maskT.b])
        dma("sp", wg2.ap[0:16, :], wg2_d[:, :], [], [wg2.b])
        dma("sp", wg2.ap[16:17, :], bgate_d[0:1, :], [], [wg2.b])
        P.op("dve", lambda e: e.memset(alT.ap, 1.0), [], alTb)
        dma("sp", bgate.ap, bgate_d[0, :].partition_broadcast(128), [], [bgate.b])

        arenaA = T(mT.ap, "arenaA")
        arbA = {(kc, sg_): Buf("arA%d_%d" % (kc, sg_)) for kc in range(8) for sg_ in range(2)}
        load_arena(arenaA, arbA, w_in, C_K, 512, 512)
        load_arena(arenaA, arbA, w_in, C_V, 1024, 1024)
        xT4 = xT.ap

        def alow_group(n):
            for kc in range(8):
                pe_mm(banks[4][0:16, :], 4, wal.ap[:, kc, :], xT4[:, kc, n * 512:(n + 1) * 512],
                      [wal.b] + xTb[n * 4:(n + 1) * 4], start=(kc == 0), stop=(kc == 7))
            act(alT.ap[0:16, n * 512:(n + 1) * 512], banks[4][0:16, :], AF.Identity, [bankb[4], balow.b], [alTb[n]],
                bias=balow.ap[0:16, :])

        for i in range(NT):
            s = xs[i % 2]
            if i >= 2:
                dma("sp", s.ap, x[i * 128:(i + 1) * 128, :], [], [s.b])
            if i % 4 == 0 and i >= 4:
                alow_group(i // 4 - 1)
            b0 = (i % 2) * 2
            for j in range(8):
                pe_tr(bk(b0 + j // 4, (j % 4) * 128, (j % 4 + 1) * 128), b0 + j // 4, s.ap[:, j * 128:(j + 1) * 128], [s.b])
            cp("act", xT4[:, 0:4, i * 128:(i + 1) * 128], bk3(b0), [bankb[b0]], [xTb[i]])
            cp("dve", xT4[:, 4:8, i * 128:(i + 1) * 128], bk3(b0 + 1), [bankb[b0 + 1]], [xTb[i]])

        gate_b = Buf("p0_done")
        P.op("pool", lambda e: e.memset(c_zero.ap, 0.0), [xTb[NT - 3]], [c_zero.b, gate_b])
        def load_after(dst, dstb, src, c0, ncols, a0):
            nkc = dst.ap.shape[1]
            for kc in range(nkc):
                dma("pool", dst.ap[:, kc, a0:a0 + ncols], src[kc * 128:(kc + 1) * 128, c0:c0 + ncols],
                    [gate_b], [dstb[(kc, a0 // 1024)]])
        wst8g = [Buf("wst8_%d" % g) for g in range(8)]
        for g in range(8):
            dma("sp", wst8.ap[:, g, :], sg_w_s[g * 128:(g + 1) * 128, :], [gate_b], [wst8g[g]])
        load_after(arena, arb, w_in, C_U, 1024, 0)
        load_after(arena, arb, w_in, C_GA, 1024, 2048)
        load_after(arena, arb, w_in, C_VS, 1024, 1024)
        load_after(arena, arb, w_in, C_GB, 1024, 3072)
        alow_group(3)

        dec2 = [cst.alloc(16, F32, "dec%d" % i) for i in range(2)]

        def gla_pass(full, ar, arbufs):
            p1.cur = p1_base
            zb = [p1.alloc(2048, F32, "zb%d" % i) for i in range(2)]
            lb = p1.alloc(2048, F32, "lb")
            Erev = p1.alloc(2048, F32, "Erev")
            kdec = [p1.alloc(1024, BF16, "kdec%d" % i) for i in range(2)]
            vsb = [p1.alloc(2048, BF16, "vsb%d" % i) for i in range(2)]
            kT = p1.alloc(4096, F32, "kT", (4, ST))
            kT_hb = [Buf("kT_h%d" % h) for h in range(4)]
            qT_hb = [Buf("qT_h%d" % h) for h in range(4)]
            temps = zb + [lb, Erev] + kdec + vsb
            if full:
                EkT = p1.alloc(2048, F32, "EkT")
                EqT = p1.alloc(2048, F32, "EqT")
                qpT = [p1.alloc(1024, BF16, "qpT%d" % i, (4, 128)) for i in range(2)]
                kpT = [p1.alloc(1024, BF16, "kpT%d" % i, (4, 128)) for i in range(2)]
                scm = p1.alloc(1024, BF16, "scm")
                on = p1.alloc(4096, F32, "on")
                on_hb = [Buf("on_h%d" % h) for h in range(4)]
                ssq_hb = [Buf("ssq_h%d" % h) for h in range(4)]
                qT = p1.alloc(4096, F32, "qT", (4, ST))
                gbT = p1.alloc(8192, F32, "gbT", (8, ST))
                gbT_vb = [Buf("gbT_v%d" % v) for v in range(8)]
                sgt = [p1.alloc(1024, F32, "sgt%d" % i) for i in range(2)]
                temps += [EkT, EqT, scm] + qpT + kpT + sgt
            P.fence([t.b for t in temps] + (on_hb + ssq_hb + gbT_vb if full else []) + kT_hb + (qT_hb if full else []))
            dma("sp", bias_bc.ap, b_in[0, C_V:C_V + 1024].partition_broadcast(128), [], [bias_bc.b])
            pb = [0]

            def arr_(kc, a):
                return arbufs[(kc, a // 1024)]

            def proj_fm(a0, t0, evac):
                bi = pb[0] % 2
                pb[0] += 1
                for kc in range(8):
                    pe_mm(bk(bi, 0, ST), bi, ar.ap[:, kc, a0:a0 + 128], xT4[:, kc, t0:t0 + ST],
                          [arr_(kc, a0)] + xTb[t0 // 128:t0 // 128 + TPS], start=(kc == 0), stop=(kc == 7))
                evac(bi)

            def proj_a(st):
                t0 = st * ST
                for hb in range(4):
                    proj_fm(512 + hb * 128, t0,
                            lambda bi, hb=hb: ts("dve", kT.ap[:, hb, :], bk(bi, 0, ST), col(bcol(C_K + hb * 128)), None, ALU.add, None,
                                                 [bankb[bi], cols.b], [kT_hb[hb]]))
                if full:
                    for hb in range(4):
                        proj_fm(hb * 128, t0,
                                lambda bi, hb=hb: ts("dve", qT.ap[:, hb, :], bk(bi, 0, ST), col(bcol(C_Q + hb * 128)), None, ALU.add, None,
                                                     [bankb[bi], cols.b], [qT_hb[hb]]))

            def proj_b(st):
                t0 = st * ST
                for vb in range(8):
                    proj_fm(2048 + vb * 128, t0,
                            lambda bi, vb=vb: act(gbT.ap[:, vb, :], bk(bi, 0, ST), AF.Silu, [bankb[bi], cols.b], [gbT_vb[vb]],
                                                  bias=col(bcol(C_R + vb * 128))))
                for vb in range(8):
                    sg = sgt[vb % 2]

                    def ev(bi, vb=vb, sg=sg):
                        act(sg.ap, bk(bi, 0, ST), AF.Sigmoid, [bankb[bi], cols.b], [sg.b], bias=col(bcol(C_GB + vb * 128)))
                        stt(gbT.ap[:, vb, :], gbT.ap[:, vb, :], col(COL_GN + vb), sg.ap, ALU.mult, ALU.mult,
                            [gbT_vb[vb], sg.b, cols.b], [gbT_vb[vb]])
                    proj_fm(3072 + vb * 128, t0, ev)

            def early(i):
                p = i % 2
                j = i % TPS
                tc0, tc1 = j * 128, (j + 1) * 128
                z, dc, kd, vs_ = zb[p], dec2[p], kdec[p], vsb[p]

                def e0():
                    pe_mm(bk(2), 2, alT.ap[:, i * 128:(i + 1) * 128], wg2.ap, [alTb[i // 4], wg2.b])
                    act(lb.ap, bk(2), AF.Exp, [bankb[2]], [lb.b], scale=-1.0)
                    act(z.ap, lb.ap, AF.Ln, [lb.b, c_one.b], [z.b], bias=c_one.ap)

                def e1():
                    for h in range(4):
                        pe_mm(bk(3, h * 128, (h + 1) * 128), 3, z.ap[:, h * 128:(h + 1) * 128], Linc.ap, [z.b, Linc.b])
                    pe_mm(bk(4), 4, Ust.ap, z.ap, [Ust.b, z.b])
                    act(dc.ap, bk3(3)[:, :, 127], AF.Exp, [bankb[3]], [dc.b])
                    if full:
                        act(EkT.ap, bk(3), AF.Exp, [bankb[3]], [EkT.b], scale=-1.0)
                        act(EqT.ap, bk(3), AF.Exp, [bankb[3], c_lns.b], [EqT.b], bias=c_lns.ap)
                    act(Erev.ap, bk(4), AF.Exp, [bankb[4]], [Erev.b])
                    if full:
                        tt("dve", kpT[p].ap, kT.ap[:, :, tc0:tc1], EkT.ap.rearrange("p (a b) -> p a b", a=4), ALU.mult,
                           kT_hb + [EkT.b], [kpT[p].b])
                        tt("dve", qpT[p].ap, qT.ap[:, :, tc0:tc1], EqT.ap.rearrange("p (a b) -> p a b", a=4), ALU.mult,
                           qT_hb + [EqT.b], [qpT[p].b])

                def e2():
                    for h in range(4):
                        pe_tr(bk(5, h * 128, (h + 1) * 128), 5, kT.ap[:, h, tc0:tc1], [kT_hb[h]])
                    tt("dve", kd.ap, bk(5), Erev.ap, ALU.mult, [bankb[5], Erev.b], [kd.b])

                def e3(halves=(0, 1)):
                    for half in halves:
                        for kc in range(8):
                            pe_mm(bk(6 + half), 6 + half, xT4[:, kc, i * 128:(i + 1) * 128],
                                  ar.ap[:, kc, 1024 + half * 512:1024 + (half + 1) * 512],
                                  [xTb[i], arr_(kc, 1024)], start=(kc == 0), stop=(kc == 7))
                        tt("dve", vs_.ap[:, half * 512:(half + 1) * 512], bk(6 + half), bias_bc.ap[:, half * 512:(half + 1) * 512],
                           ALU.add, [bankb[6 + half], bias_bc.b], [vs_.b])
                return [e0, e1, e2, e3]

            def late(i):
                p = i % 2
                j = i % TPS
                tc0, tc1 = j * 128, (j + 1) * 128
                dc, kd, vs_ = dec2[p], kdec[p], vsb[p]
                OB = (1, 2)
                TB = (3, 4)
                UB = (5, 0)

                def l0():
                    if not full:
                        return
                    for h in range(4):
                        pe_mm(bk(0, h * 128, (h + 1) * 128), 0, kpT[p].ap[:, h, :], qpT[p].ap[:, h, :], [kpT[p].b, qpT[p].b])
                    tt("dve", scm.ap, bk(0), mask4.ap, ALU.mult, [bankb[0], mask4.b], [scm.b])

                def l1():
                    if not full:
                        return
                    for h in range(4):
                        ob, oc = OB[h // 2], (h % 2) * 256
                        pe_mm(bk(ob, oc, oc + 256), ob, scm.ap[:, h * 128:(h + 1) * 128], vs_.ap[:, h * 256:(h + 1) * 256],
                              [scm.b, vs_.b], start=True, stop=False)
                        pe_mm(bk(ob, oc, oc + 256), ob, qpT[p].ap[:, h, :], S_bf.ap[:, h, :], [qpT[p].b, S_bf.b],
                              start=False, stop=True)
                    for h in range(4):
                        ob, oc = OB[h // 2], (h % 2) * 256
                        act(on.ap[:, h * 256:(h + 1) * 256], bk(ob, oc, oc + 256), AF.Square, [bankb[ob]], [on_hb[h], ssq_hb[h]],
                            accum=ssq.ap[:, h:h + 1])
                    act(rstd.ap, ssq.ap, AF.Ln, ssq_hb + [c_eps.b], [rstd.b], bias=c_eps.ap, scale=1.0 / 256.0)
                    act(rstd.ap, rstd.ap, AF.Exp, [rstd.b], [rstd.b], scale=-0.5)
                    for h in range(4):
                        ob, oc = OB[h // 2], (h % 2) * 256
                        act(on.ap[:, h * 256:(h + 1) * 256], bk(ob, oc, oc + 256), AF.Identity, [bankb[ob], rstd.b], [on_hb[h]],
                            scale=rstd.ap[:, h:h + 1])

                def l2():
                    if not full:
                        return
                    for vb in range(8):
                        tb = TB[vb // 4]
                        pe_tr(bk(tb, (vb % 4) * 128, (vb % 4 + 1) * 128), tb, on.ap[:, vb * 128:(vb + 1) * 128], [on_hb[vb // 2]])
                    on3 = on.ap.rearrange("p (a b) -> p a b", a=8)
                    for hf in range(2):
                        tb = TB[hf]
                        tt("dve", on3[:, hf * 4:(hf + 1) * 4, :], bk3(tb), gbT.ap[:, hf * 4:(hf + 1) * 4, tc0:tc1], ALU.mult,
                           [bankb[tb]] + gbT_vb[hf * 4:(hf + 1) * 4], [on_hb[2 * hf], on_hb[2 * hf + 1]])
                    msl = mT.ap[:, :, i * 128:(i + 1) * 128]
                    tt("dve", msl, on3, msl, ALU.add, on_hb + [mTb[i]], [mTb[i]])

                def l3():
                    for h in range(4):
                        ub, uc = UB[h // 2], (h % 2) * 256
                        pe_mm(bk(ub, uc, uc + 256), ub, kd.ap[:, h * 128:(h + 1) * 128], vs_.ap[:, h * 256:(h + 1) * 256],
                              [kd.b, vs_.b])
                    for h in range(4):
                        ub, uc = UB[h // 2], (h % 2) * 256
                        stt(S.ap[:, h, :], S.ap[:, h, :], dc.ap[:, h:h + 1], bk(ub, uc, uc + 256), ALU.mult, ALU.add,
                            [S_hb[h], dc.b, bankb[ub]], [S_hb[h]])
                    if full:
                        cp("dve", S_bf.ap, S.ap, S_hb, [S_bf.b])
                    else:
                        tt("dve", Dacc.ap, Dacc.ap, dc.ap, ALU.mult, [Dacc.b, dc.b], [Dacc.b])
                return [l0, l1, l2, l3]

            for i in range(NT + 1):
                if i < NT and i % TPS == 0:
                    proj_a(i // TPS)
                E = early(i) if i < NT else [lambda *a: None] * 4
                L = late(i - 1) if i >= 1 else [lambda *a: None] * 4
                E[0]()
                L[0]()
                E[3]((0,))
                E[1]()
                L[1]()
                E[3]((1,))
                E[2]()
                L[3]()
                L[2]()
                if full and i < NT and i % TPS == 0:
                    proj_b(i // TPS)

        P.op("dve", lambda e: e.memset(S.ap, 0.0), [], S_hb)
        memset("dve", Dacc, 1.0)
        gla_pass(False, arenaA, arbA)
        p1.cur = p1_base
        pay = p1.alloc(4128, F32, "pay")
        P.fence([pay.b])
        cp("dve", pay.ap[:, 0:1024], S.ap.rearrange("p a b -> p (a b)"), S_hb, [pay.b])
        cp("dve", pay.ap[:, 1024:1028], Dacc.ap, [Dacc.b, pay.b], [pay.b])
        bcc_in, bcc_out = Buf("cc_in"), Buf("cc_out")
        dma("sp", cc_in.ap()[:, :], pay.ap[:, 0:1028], [pay.b], [bcc_in])
        cc_tok = P.ext("pool", lambda e: e.collective_compute("AllGather", ALU.bypass, replica_groups=[[0, 1, 2, 3], [4, 5, 6, 7]],
                                                              ins=[cc_in.ap().opt()], outs=[cc_out.ap().opt()]),
                       [bcc_in], [bcc_out])
        P._wait("pool", cc_tok)
        p1.cur = p1_base
        gst3 = [p1.alloc(4128, F32, "gst%d" % j) for j in range(3)]
        mS = p1.alloc(4096, F32, "mS", (4, 256))
        Sin = p1.alloc(4096, F32, "Sin", (4, 256))
        P.fence([t.b for t in gst3] + [mS.b, Sin.b])
        for jq in range(3):
            dma("sp", gst3[jq].ap[:, 0:1028], cc_out.ap()[jq * 128:(jq + 1) * 128, :], [bcc_out], [gst3[jq].b])
        memset("dve", Sin, 0.0)
        for jq in range(3):
            gst = gst3[jq]
            ts("dve", dprime.ap, gst.ap[:, 1024:1028], cmask.ap[:, jq:jq + 1], cmask.ap[:, 4 + jq:5 + jq], ALU.mult, ALU.add,
               [gst.b, cmask.b], [dprime.b])
            ts("dve", mS.ap.rearrange("p a b -> p (a b)"), gst.ap[:, 0:1024], cmask.ap[:, jq:jq + 1], None, ALU.mult, None,
               [gst.b, cmask.b], [mS.b])
            for h in range(4):
                stt(Sin.ap[:, h, :], Sin.ap[:, h, :], dprime.ap[:, h:h + 1], mS.ap[:, h, :], ALU.mult, ALU.add,
                    [Sin.b, dprime.b, mS.b], [Sin.b])
        cp("dve", S.ap, Sin.ap, [Sin.b], S_hb)
        cp("dve", S_bf.ap, Sin.ap, [Sin.b], [S_bf.b])

        p1.cur = p1_base
        wsT = p1.alloc(2048, BF16, "wsT", (8, 128))
        Cg = p1.alloc(4096, F32, "Cg", (8, 128))
        wst = p1.alloc(512, F32, "wst")
        wsf = p1.alloc(512, F32, "wsf")
        onesf = p1.alloc(512, F32, "onesf")
        ugT2 = [p1.alloc(8192, F32, "ugT%d" % i, (8, ST)) for i in range(2)]
        ugT_cb = [[Buf("ugT%d_c%d" % (i, c)) for c in range(8)] for i in range(2)]
        sgb1 = [p1.alloc(1024, F32, "sgb1_%d" % i) for i in range(2)]
        zt2 = [p1.alloc(4096, F32, "zt%d" % i) for i in range(2)]
        vn2 = [p1.alloc(2048, BF16, "vn%d" % i) for i in range(2)]
        yat2 = [p1.alloc(4096, F32, "yat%d" % i, (8, 128)) for i in range(2)]
        yat_gb = [[Buf("yat%d_g%d" % (i, g)) for g in range(8)] for i in range(2)]
        P.fence([wsT.b, Cg.b, wst.b, wsf.b, onesf.b, sgb1[0].b, sgb1[1].b]
                + [t.b for t in zt2 + vn2 + yat2] + [b_ for l_ in ugT_cb for b_ in l_])
        P.fence(mTb)
        dma("sp", bias_bc.ap, b_in[0, C_VS:C_VS + 1024].partition_broadcast(128), [], [bias_bc.b])
        dma("sp", Cg.ap.rearrange("p a b -> p (a b)"), sg_b_s[0, :].partition_broadcast(128), [], [Cg.b])
        memset("dve", onesf, 1.0)
        wsf8 = T(yat2[1].ap, "wsf8")
        P.fence([wsf8.b])
        for g in range(8):
            tt("dve", wst8.ap[:, g, :], wst8.ap[:, g, :], maskT.ap, ALU.mult, [wst8.b, wst8g[g], maskT.b], [wst8.b])
        for g in range(8):
            tb = 2 + g // 4
            pe_tr(bk(tb, (g % 4) * 128, (g % 4 + 1) * 128), tb, wst8.ap[:, g, :], [wst8.b])
        for hf in range(2):
            cp("act", wsf8.ap[:, hf * 4:(hf + 1) * 4, :], bk3(2 + hf), [bankb[2 + hf]], [wsf8.b])
            cp("dve", wsT.ap[:, hf * 4:(hf + 1) * 4, :], bk3(2 + hf), [bankb[2 + hf]], [wsT.b])
        for g in range(8):
            tb = 4 + g // 4
            pe_mm(bk(tb, (g % 4) * 128, (g % 4 + 1) * 128), tb, onesf.ap, wsf8.ap[:, g, :], [onesf.b, wsf8.b])
        for g in range(8):
            tb = 4 + g // 4
            stt(Cg.ap[:, g, :], bk(tb, (g % 4) * 128, (g % 4 + 1) * 128), col(COL_SGB + g), Cg.ap[:, g, :], ALU.mult, ALU.add,
                [bankb[tb], cols.b, Cg.b], [Cg.b])
        P.fence([b_ for l_ in yat_gb for b_ in l_])
        pb1 = [0]

        def b1_stproj(st, part):
            t0 = st * ST
            ugT = ugT2[st % 2]
            for cb in (range(8) if part == 0 else []):
                bi = pb1[0] % 2
                pb1[0] += 1
                for kc in range(8):
                    pe_mm(bk(bi, 0, ST), bi, arena.ap[:, kc, cb * 128:(cb + 1) * 128], xT4[:, kc, t0:t0 + ST],
                          [arr(kc, cb * 128)] + xTb[t0 // 128:t0 // 128 + TPS], start=(kc == 0), stop=(kc == 7))
                act(ugT.ap[:, cb, :], bk(bi, 0, ST), AF.Gelu_apprx_tanh, [bankb[bi], cols.b], [ugT_cb[st % 2][cb]], bias=col(bcol(C_U + cb * 128)))
            for cb in (range(8) if part == 1 else []):
                bi = pb1[0] % 2
                pb1[0] += 1
                for kc in range(8):
                    pe_mm(bk(bi, 0, ST), bi, arena.ap[:, kc, 2048 + cb * 128:2048 + (cb + 1) * 128], xT4[:, kc, t0:t0 + ST],
                          [arr(kc, 2048)] + xTb[t0 // 128:t0 // 128 + TPS], start=(kc == 0), stop=(kc == 7))
                sg = sgb1[cb % 2]
                act(sg.ap, bk(bi, 0, ST), AF.Sigmoid, [bankb[bi], cols.b], [sg.b], bias=col(bcol(C_GA + cb * 128)))
                tt("dve", ugT.ap[:, cb, :], ugT.ap[:, cb, :], sg.ap, ALU.mult, [ugT_cb[st % 2][cb], sg.b], [ugT_cb[st % 2][cb]])

        def b1_vs(i):
            zt, vn = zt2[i % 2], vn2[i % 2]
            vb0 = 2 if i % 2 == 0 else 6
            for half in range(2):
                for kc in range(8):
                    pe_mm(bk(vb0 + half), vb0 + half, xT4[:, kc, i * 128:(i + 1) * 128],
                          arena.ap[:, kc, 1024 + half * 512:1024 + (half + 1) * 512],
                          [xTb[i], arr(kc, 1024)], start=(kc == 0), stop=(kc == 7))
                tt("dve", zt.ap[:, half * 512:(half + 1) * 512], bk(vb0 + half), bias_bc.ap[:, half * 512:(half + 1) * 512],
                   ALU.add, [bankb[vb0 + half], bias_bc.b], [zt.b])
            act(zt.ap, zt.ap, AF.Gelu_apprx_tanh, [zt.b], [zt.b])
            for half in range(2):
                P.op("dve", lambda e, half=half: e.bn_stats(out=st6.ap[:, half * 6:(half + 1) * 6],
                                                           in_=zt.ap[:, half * 512:(half + 1) * 512]), [zt.b], [st6_hb[half]])
            P.op("dve", lambda e: e.bn_aggr(out=mv.ap, in_=st6.ap), st6_hb, [mv.b])
            if i >= NT - 3:
                act(rstd.ap[:, 0:1], mv.ap[:, 1:2], AF.Ln, [mv.b, c_eps.b], [rstd.b], bias=c_eps.ap)
                act(rstd.ap[:, 0:1], rstd.ap[:, 0:1], AF.Exp, [rstd.b], [rstd.b], scale=-0.5)
            else:
                ts("dve", rstd.ap[:, 0:1], mv.ap[:, 1:2], LN_EPS, None, ALU.add, None, [mv.b], [rstd.b])
                tt("pool", rstd.ap[:, 0:1], rstd.ap[:, 0:1], c_mh.ap[:, 0:1], ALU.pow, [rstd.b, c_mh.b], [rstd.b])
            ts("dve", vn.ap, zt.ap, mv.ap[:, 0:1], rstd.ap[:, 0:1], ALU.subtract, ALU.mult, [zt.b, mv.b, rstd.b], [vn.b])

        def b1_mix(i):
            vn, yat, ugT = vn2[i % 2], yat2[i % 2], ugT2[(i // TPS) % 2]
            j = i % TPS
            for g in range(8):
                mb = 4 + g // 4
                pe_mm(bk(mb, (g % 4) * 128, (g % 4 + 1) * 128), mb, vn.ap[:, g * 128:(g + 1) * 128], wsT.ap[:, g, :],
                      [vn.b, wsT.b])
            for g in range(8):
                mb = 4 + g // 4
                stt(yat.ap[:, g, :], bk(mb, (g % 4) * 128, (g % 4 + 1) * 128), col(COL_SGG + g), Cg.ap[:, g, :],
                    ALU.mult, ALU.add, [bankb[mb], cols.b, Cg.b], [yat_gb[i % 2][g]])
            tt("dve", mT.ap[:, :, i * 128:(i + 1) * 128], yat.ap, ugT.ap[:, :, j * 128:(j + 1) * 128], ALU.mult,
               yat_gb[i % 2] + ugT_cb[(i // TPS) % 2], [mTb[i]])

        b1_stproj(0, 0)
        b1_stproj(0, 1)
        b1_vs(0)
        for i in range(NT):
            if i // TPS + 1 < NST:
                b1_stproj(i // TPS + 1, i % TPS)
            b1_mix(i)
            if i + 1 < NT:
                b1_vs(i + 1)
            if i == NT - 3:
                load_arena(arena, arb, w_in, C_K, 512, 512)
                load_arena(arena, arb, w_in, C_Q, 512, 0)
                load_arena(arena, arb, w_in, C_R, 1024, 2048)
            if i == NT - 2:
                load_arena(arena, arb, w_in, C_V, 1024, 1024)

        p1.cur = p1_base
        gla_pass(True, arena, arb)

        if debug:
            dbgb = Buf("dbg")
            for kc in range(8):
                dma("pool", dbg[:, kc * NTOK:(kc + 1) * NTOK], mT.ap[:, kc, :], mTb, [dbgb])

        W1 = carve(OFF_R1, 65536, BF16, "W1", (8, 4096))
        W2 = carve(OFF_R1 + 65536, 65536, BF16, "W2", (32, 1024))
        Wo = carve(OFF_R1 + 131072, 16384, BF16, "Wo", (8, 1024))
        w1b = {(kc, s): Buf("w1_%d_%d" % (kc, s)) for kc in range(8) for s in range(4)}
        w2b = {(kc, 0): Buf("w2_%d" % kc) for kc in range(32)}
        wob = {(kc, 0): Buf("wo_%d" % kc) for kc in range(8)}
        p2 = Bump(OFF_W, SB_LIMIT)
        xh = [p2.alloc(4096, F32, "xh%d" % i) for i in range(4)]
        rt = [p2.alloc(1024, F32, "rt%d" % i) for i in range(3)]
        lnA = p2.alloc(4096, F32, "lnA")
        lnB = p2.alloc(4096, F32, "lnB")
        aT = [cst.alloc(512, BF16, "aT%d" % i) for i in range(3)]
        P.fence(list(w1b.values()) + list(w2b.values()) + list(wob.values()) + [t.b for t in xh] + [lnA.b, lnB.b]
                + [t.b for t in rt])
        for i in range(4):
            dma("sp", xh[i].ap, x[i * 128:(i + 1) * 128, :], [], [xh[i].b])
        ln_cur = [ln1_g]
        dma("sp", lnA.ap, ln1_g[0, :].partition_broadcast(128), [], [lnA.b])
        dma("sp", lnB.ap, ln1_b[0, :].partition_broadcast(128), [], [lnB.b])
        xdep = [t.b for t in xh] + [lnA.b, lnB.b]
        for kc in range(8):
            dma("pool", Wo.ap[:, kc, :], w_out[kc * 128:(kc + 1) * 128, :], xdep, [wob[(kc, 0)]])
        for seg in range(4):
            for kc in range(8):
                dma("pool", W1.ap[:, kc, seg * 1024:(seg + 1) * 1024], w_ff1[kc * 128:(kc + 1) * 128, seg * 1024:(seg + 1) * 1024],
                    [], [w1b[(kc, seg)]])
            for kc in range(seg * 8, (seg + 1) * 8):
                dma("pool", W2.ap[:, kc, :], w_ff2[kc * 128:(kc + 1) * 128, :], [], [w2b[(kc, 0)]])
        outb = Buf("out")

        def ln_load(g_d, b_d):
            if ln_cur[0] is not g_d:
                dma("sp", lnA.ap, g_d[0, :].partition_broadcast(128), [], [lnA.b])
                dma("sp", lnB.ap, b_d[0, :].partition_broadcast(128), [], [lnB.b])
                ln_cur[0] = g_d

        def layer_norm(xt, g_d, b_d):
            for half in range(2):
                P.op("dve", lambda e, half=half: e.bn_stats(out=st6.ap[:, half * 6:(half + 1) * 6],
                                                           in_=xt.ap[:, half * 512:(half + 1) * 512]), [xt.b], [st6_hb[half]])
            P.op("dve", lambda e: e.bn_aggr(out=mv.ap, in_=st6.ap), st6_hb, [mv.b])
            act(rstd.ap[:, 0:1], mv.ap[:, 1:2], AF.Ln, [mv.b, c_eps.b], [rstd.b], bias=c_eps.ap)
            act(rstd.ap[:, 0:1], rstd.ap[:, 0:1], AF.Exp, [rstd.b], [rstd.b], scale=-0.5)
            ts("dve", xt.ap, xt.ap, mv.ap[:, 0:1], rstd.ap[:, 0:1], ALU.subtract, ALU.mult, [xt.b, mv.b, rstd.b], [xt.b])
            ln_load(g_d, b_d)
            tt("dve", xt.ap, xt.ap, lnA.ap, ALU.mult, [xt.b, lnA.b], [xt.b])
            tt("dve", xt.ap, xt.ap, lnB.ap, ALU.add, [xt.b, lnB.b], [xt.b])

        def pre_chunks(st):
            chunks = []
            for j in range(TPS):
                i = st * TPS + j
                xt = xh[i % 4]

                def c_op(half, i=i, xt=xt):
                    if half == 0 and i >= 4:
                        dma("sp", xt.ap, x[i * 128:(i + 1) * 128, :], [], [xt.b])
                    for kc in range(8):
                        pe_mm(bk(7), 7, mT.ap[:, kc, i * 128:(i + 1) * 128], Wo.ap[:, kc, half * 512:(half + 1) * 512],
                              [mTb[i], wob[(kc, 0)]], start=(kc == 0), stop=(kc == 7))
                    sl = xt.ap[:, half * 512:(half + 1) * 512]
                    stt(sl, sl, ALPHA, bk(7), ALU.mult, ALU.add, [xt.b, bankb[7]], [xt.b])
                    if half == 1:
                        layer_norm(xt, ln1_g, ln1_b)

                def c_tr(hf, i=i, xt=xt):
                    for db in range(hf * 4, (hf + 1) * 4):
                        pe_tr(bk(7, (db % 4) * 128, (db % 4 + 1) * 128), 7, xt.ap[:, db * 128:(db + 1) * 128], [xt.b])
                    cp("act", mT.ap[:, hf * 4:(hf + 1) * 4, i * 128:(i + 1) * 128], bk3(7), [bankb[7]], [mTb[i]])

                chunks.append((j * 10 + 8, lambda f=c_op: f(0)))
                chunks.append((j * 10 + 9, lambda f=c_op: f(1)))
                chunks.append((j * 10 + 16, lambda f=c_tr: f(0)))
                chunks.append((j * 10 + 17, lambda f=c_tr: f(1)))
            return chunks

        def finish_tile(i):
            xt = xh[i % 4]
            layer_norm(xt, ln2_g, ln2_b)
            dma("sp", out[i * 128:(i + 1) * 128, :], xt.ap, [xt.b], [outb])

        for _, f in pre_chunks(0):
            f()
        for st in range(NST):
            sched = {}
            if st >= 1:
                sched[2] = [lambda i=(st - 1) * TPS: finish_tile(i)]
                sched[4] = [lambda i=(st - 1) * TPS + 1: finish_tile(i)]
            if st + 1 < NST:
                sched[5] = [lambda: ln_load(ln1_g, ln1_b)]
                for fbk_, f in pre_chunks(st + 1):
                    sched.setdefault(fbk_, []).append(f)
            sched.setdefault(28, []).append(lambda: ln_load(ln2_g, ln2_b))
            hsl = [mTb[st * TPS + j] for j in range(TPS)]

            def ff1(fb):
                fbk = 4 + fb % 3
                for kc in range(8):
                    pe_mm(bk(fbk, 0, ST), fbk, W1.ap[:, kc, fb * 128:(fb + 1) * 128], mT.ap[:, kc, st * ST:(st + 1) * ST],
                          [w1b[(kc, fb // 8)]] + hsl, start=(kc == 0), stop=(kc == 7))
                r_ = rt[fb % 3]
                a_ = aT[fb % 3]
                act(r_.ap, bk(fbk, 0, ST), AF.Relu, [bankb[fbk]], [r_.b])
                tt("dve", a_.ap, r_.ap, r_.ap, ALU.mult, [r_.b], [a_.b])

            def ff2(fb):
                a_ = aT[fb % 3]
                for j in range(TPS):
                    for half in range(2):
                        ab = j * 2 + half
                        pe_mm(bk(ab), ab, a_.ap[:, j * 128:(j + 1) * 128], W2.ap[:, fb, half * 512:(half + 1) * 512],
                              [a_.b, w2b[(fb, 0)]], start=(fb == 0), stop=(fb == 31))

            ff1(0)
            ff1(1)
            for fb in range(32):
                if fb + 2 < 32:
                    ff1(fb + 2)
                ff2(fb)
                for f in sched.get(fb, []):
                    f()
            for j in range(TPS):
                i = st * TPS + j
                xt = xh[i % 4]
                for half in range(2):
                    sl = xt.ap[:, half * 512:(half + 1) * 512]
                    stt(sl, sl, ALPHA, bk(j * 2 + half), ALU.mult, ALU.add, [xt.b, bankb[j * 2 + half]], [xt.b])
        for j in range(TPS):
            finish_tile((NST - 1) * TPS + j)

        for k in range(N_DMA_SEMS):
            if P.dma_cnt[k]:
                P._wait("sp", ("dma", k, P.dma_cnt[k]))
        P.emit(block, esem, dsems)
    return nc


_NC_CACHE = {}


def _prep_inputs(inputs):
    f = lambda a: np.ascontiguousarray(np.asarray(a, dtype=np.float32))
    x = f(inputs["x"]).reshape(2 * 8192, D)
    shared = {
        "w_in": f(inputs["w_in"]).reshape(D, D_IN),
        "b_in": f(inputs["b_in"]).reshape(1, D_IN),
        "sg_ln_g": f(inputs["sg_ln_g"]).reshape(1, D),
        "sg_ln_b": f(inputs["sg_ln_b"]).reshape(1, D),
        "sg_w_s": f(inputs["sg_w_s"]).reshape(8 * 128, 128),
        "sg_b_s": f(inputs["sg_b_s"]).reshape(1, D),
        "gla_w_gate2": f(inputs["gla_w_gate2"]).reshape(16, 512),
        "gla_b_gate": f(inputs["gla_b_gate"]).reshape(1, 512),
        "gla_norm_g": f(inputs["gla_norm_g"]).reshape(1, D),
        "w_out": f(inputs["w_out"]).reshape(D, D),
        "ln1_g": f(inputs["ln1_g"]).reshape(1, D),
        "ln1_b": f(inputs["ln1_b"]).reshape(1, D),
        "w_ff1": f(inputs["w_ff1"]).reshape(D, D_FF),
        "w_ff2": f(inputs["w_ff2"]).reshape(D_FF, D),
        "ln2_g": f(inputs["ln2_g"]).reshape(1, D),
        "ln2_b": f(inputs["ln2_b"]).reshape(1, D),
    }
    in_maps = []
    for c in range(8):
        qc = c % 4
        cm = np.zeros((128, 8), np.float32)
        for j in range(4):
            cm[:, j] = 1.0 if j < qc else 0.0
            cm[:, 4 + j] = 0.0 if j < qc else 1.0
        m = dict(shared)
        m["x"] = np.ascontiguousarray(x[c * NTOK:(c + 1) * NTOK])
        m["cmask"] = cm
        in_maps.append(m)
    return in_maps


def kernel(**inputs):
    if "nc" not in _NC_CACHE:
        _NC_CACHE["nc"] = build_nc()
    nc = _NC_CACHE["nc"]
    in_maps = _prep_inputs(inputs)
    res = run_bass_kernel_spmd(nc, in_maps, core_ids=list(range(8)))
    outs = [np.asarray(res.results[c]["out"], dtype=np.float32) for c in range(8)]
    return np.concatenate(outs, axis=0).reshape(2, 8192, D)
```

```python
import math
from contextlib import ExitStack

import numpy as np
import concourse.bass as bass
import concourse.mybir as mybir
from concourse.bass_utils import run_bass_kernel_spmd

F32 = mybir.dt.float32
BF16 = mybir.dt.bfloat16
AF = mybir.ActivationFunctionType
ALU = mybir.AluOpType

ENGS = ["pe", "act", "dve", "pool", "sp"]
N_DMA_SEMS = 24

D = 1024
NTOK = 2048
NT = NTOK // 128
ST = 256
NST = NTOK // ST
TPS = ST // 128
D_IN = 7184
C_U, C_VS, C_Q, C_K, C_V, C_R, C_AL, C_GA, C_GB = 0, 1024, 2048, 2560, 3072, 4096, 5120, 5136, 6160
D_FF = 4096
LN_EPS = 1e-5
ALPHA = 2.0 ** 0.25
GATE_TEMP = 16.0
Q_SCALE = 128.0 ** -0.5


def bcol(c0):
    return c0 // 128 if c0 < C_AL else 40 + (c0 - C_GA) // 128


COL_SGG, COL_SGB, COL_GN = 56, 64, 72


class Buf:
    def __init__(self, name, excl=False):
        self.name = name
        self.excl = excl
        self.w = None
        self.w_read = False
        self.r = {}


class Prog:
    def __init__(self):
        self.q = {e: [] for e in ENGS}
        self.seen = {e: {} for e in ENGS}
        self.dma_cnt = [0] * (N_DMA_SEMS + 1)
        self.dma_rr = 0
        self.dma_rr_pool = 0
        self.last_op = {e: None for e in ENGS}

    def _wait(self, eng, tok):
        if tok is None:
            return
        if tok[0] == "eng":
            _, e2, idx = tok
            key = ("eng", e2)
            if self.seen[eng].get(key, -1) >= idx:
                return
            self.seen[eng][key] = idx
            self.q[e2][idx]["inc"] = True
            self.q[eng].append(dict(kind="wait", tok=tok))
        else:
            _, k, val = tok
            key = ("dma", k)
            if self.seen[eng].get(key, -1) >= val:
                return
            self.seen[eng][key] = val
            self.q[eng].append(dict(kind="wait", tok=tok))

    def _deps(self, eng, reads, writes, is_dma):
        toks = []
        writes = list(writes)
        excl_reads = []
        for b in reads:
            if b.excl:
                if b not in writes and b not in excl_reads:
                    excl_reads.append(b)
                continue
            if b.w is not None:
                toks.append(b.w)
        for b in writes:
            if b.w is not None:
                toks.append(b.w)
            toks.extend(b.r.values())
        for b in excl_reads:
            t = b.w
            if t is None:
                continue
            if t[0] == "eng" and t[1] == eng and not is_dma and b.w_read:
                continue
            toks.append(t)
        out = []
        for t in toks:
            if t[0] == "eng" and t[1] == eng and not is_dma and eng == "pe":
                continue
            out.append(t)
        return out, writes, excl_reads

    def op(self, eng, fn, reads=(), writes=()):
        reads = [b for b in reads if b is not None]
        writes = [b for b in writes if b is not None]
        toks, writes2, excl_reads = self._deps(eng, reads, writes, False)
        for t in toks:
            self._wait(eng, t)
        idx = len(self.q[eng])
        self.q[eng].append(dict(kind="op", fn=fn, inc=False))
        tok = ("eng", eng, idx)
        self.last_op[eng] = tok
        for b in writes2:
            b.w = tok
            b.w_read = False
            b.r = {}
        for b in excl_reads:
            b.w = tok
            b.w_read = True
            b.r = {}
        for b in reads:
            if b not in writes2 and b not in excl_reads:
                b.r[eng] = tok
        return tok

    def dma(self, eng, fn, reads=(), writes=()):
        reads = [b for b in reads if b is not None]
        writes = [b for b in writes if b is not None]
        toks, writes2, _er = self._deps(eng, reads, writes, True)
        assert not _er
        for t in toks:
            self._wait(eng, t)
        half = N_DMA_SEMS // 2
        if eng == "pool":
            k = half + self.dma_rr_pool
            self.dma_rr_pool = (self.dma_rr_pool + 1) % half
        else:
            k = self.dma_rr
            self.dma_rr = (self.dma_rr + 1) % half
        prev = self.dma_cnt[k]
        if prev > 0:
            self._wait(eng, ("dma", k, prev))
        self.dma_cnt[k] = prev + 16
        tok = ("dma", k, prev + 16)
        self.q[eng].append(dict(kind="dma", fn=fn, k=k))
        for b in writes2:
            b.w = tok
            b.r = {}
        for b in reads:
            if b not in writes2:
                b.r[("dma", k)] = tok
        return tok

    def ext(self, eng, fn, reads=(), writes=()):
        toks, writes2, _er = self._deps(eng, list(reads), list(writes), True)
        for t in toks:
            self._wait(eng, t)
        k = N_DMA_SEMS
        self.dma_cnt[k] += 1
        tok = ("dma", k, self.dma_cnt[k])
        self.q[eng].append(dict(kind="ext", fn=fn, k=k))
        for b in writes2:
            b.w = tok
            b.r = {}
        for b in reads:
            if b not in writes2:
                b.r[("dma", k)] = tok
        return tok

    def fence_tokens(self):
        toks = {}
        for e in ENGS:
            if self.last_op[e] is not None:
                toks[("f", e)] = self.last_op[e]
        for k in range(N_DMA_SEMS // 2):
            if self.dma_cnt[k]:
                toks[("fd", k)] = ("dma", k, self.dma_cnt[k])
        return toks

    def fence(self, bufs):
        toks = self.fence_tokens()
        for b in bufs:
            b.w = None
            b.w_read = False
            b.r = dict(toks)

    def wait_all(self, eng, bufs):
        for b in bufs:
            if b.w is not None:
                self._wait(eng, b.w)

    def emit(self, block, esem, dsems):
        cnt_at = {}
        for e in ENGS:
            c = 0
            for i, r in enumerate(self.q[e]):
                if r["kind"] == "op" and r["inc"]:
                    c += 1
                    cnt_at[(e, i)] = c
        qs = self.q

        def run(e, engine):
            for r in qs[e]:
                if r["kind"] == "wait":
                    t = r["tok"]
                    if t[0] == "eng":
                        engine.wait_ge(esem[t[1]], cnt_at[(t[1], t[2])])
                    else:
                        engine.wait_ge(dsems[t[1]], t[2])
                elif r["kind"] == "op":
                    inst = r["fn"](engine)
                    if r["inc"]:
                        inst.then_inc(esem[e], 1)
                elif r["kind"] == "ext":
                    inst = r["fn"](engine)
                    inst.then_inc(dsems[r["k"]], 1)
                else:
                    inst = r["fn"](engine)
                    inst.then_inc(dsems[r["k"]], 16)

        @block.tensor
        def _(eng):
            run("pe", eng)

        @block.scalar
        def _(eng):
            run("act", eng)

        @block.vector
        def _(eng):
            run("dve", eng)

        @block.gpsimd
        def _(eng):
            run("pool", eng)

        @block.sync
        def _(eng):
            run("sp", eng)


class T:
    def __init__(self, ap, name, buf=None):
        self.ap = ap
        self.b = buf if buf is not None else Buf(name)

    def __getitem__(self, k):
        return self.ap[k]


SB_LIMIT = 212480


def build_nc(debug=False):
    nc = bass.Bass("TRN2", target_bir_lowering=False)

    def din(name, shape):
        return nc.dram_tensor(name, shape, F32, kind="ExternalInput").ap()

    x = din("x", [NTOK, D])
    w_in = din("w_in", [D, D_IN])
    b_in = din("b_in", [1, D_IN])
    sg_ln_g = din("sg_ln_g", [1, D])
    sg_ln_b = din("sg_ln_b", [1, D])
    sg_w_s = din("sg_w_s", [8 * 128, 128])
    sg_b_s = din("sg_b_s", [1, D])
    wg2_d = din("gla_w_gate2", [16, 512])
    bgate_d = din("gla_b_gate", [1, 512])
    gnorm_d = din("gla_norm_g", [1, D])
    w_out = din("w_out", [D, D])
    ln1_g = din("ln1_g", [1, D])
    ln1_b = din("ln1_b", [1, D])
    w_ff1 = din("w_ff1", [D, D_FF])
    w_ff2 = din("w_ff2", [D_FF, D])
    ln2_g = din("ln2_g", [1, D])
    ln2_b = din("ln2_b", [1, D])
    cmask_d = din("cmask", [128, 8])
    out = nc.dram_tensor("out", [NTOK, D], F32, kind="ExternalOutput").ap()
    dbg = None
    if debug:
        dbg = nc.dram_tensor("dbg", [128, 8 * NTOK], F32, kind="ExternalOutput").ap()
    cc_in = nc.dram_tensor("cc_in", [128, 1028], F32)
    cc_out = nc.dram_tensor("cc_out", [512, 1028], F32)

    P = Prog()
    with ExitStack() as es:
        R = es.enter_context(nc.sbuf_tensor("R", [128, SB_LIMIT // 4], F32))
        banks = [es.enter_context(nc.psum_tensor("bank%d" % i, [128, 512], F32)) for i in range(8)]
        bankb = [Buf("bank%d" % i, excl=True) for i in range(8)]
        esem = {e: es.enter_context(nc.semaphore("s_" + e)) for e in ["pe", "act", "dve", "pool"]}
        dsems = [es.enter_context(nc.semaphore("d%d" % i)) for i in range(N_DMA_SEMS + 1)]
        block = es.enter_context(nc.Block())

        def carve(off, nbytes, dtype, name, shape=None, parts=128, buf=None):
            assert off % 4 == 0 and nbytes % 4 == 0 and off + nbytes <= SB_LIMIT, (name, off, nbytes)
            ap = R[0:parts, off // 4:(off + nbytes) // 4]
            if dtype is BF16:
                ap = ap.bitcast(BF16)
            if shape is not None:
                assert len(shape) == 2
                ap = ap.rearrange("p (a b) -> p a b", a=shape[0])
            return T(ap, name, buf)

        class Bump:
            def __init__(self, start, end):
                self.start, self.end, self.cur = start, end, start

            def alloc(self, nbytes, dtype, name, shape=None, parts=128):
                t = carve(self.cur, nbytes, dtype, name, shape, parts)
                self.cur += (nbytes + 31) // 32 * 32
                assert self.cur <= self.end, (name, self.cur, self.end)
                return t

        OFF_MT = 0
        OFF_CONST = 32768
        OFF_R1 = 36864
        OFF_W = OFF_R1 + 147456
        mT = carve(OFF_MT, 32768, BF16, "mT", (8, NTOK))
        mTb = [Buf("mT%d" % i) for i in range(NT)]
        cst = Bump(OFF_CONST, OFF_R1)
        ident = cst.alloc(512, F32, "ident")
        cols = cst.alloc(80 * 4, F32, "cols")
        c_one = cst.alloc(4, F32, "c_one")
        c_lns = cst.alloc(4, F32, "c_lns")
        c_mh = cst.alloc(16, F32, "c_mh")
        c_zero = cst.alloc(4, F32, "c_zero")
        c_eps = cst.alloc(4, F32, "c_eps")
        cmask = cst.alloc(32, F32, "cmask")
        balow = cst.alloc(4, F32, "balow")
        st6 = cst.alloc(48, F32, "st6")
        st6_hb = [Buf("st6_h%d" % h) for h in range(2)]
        mv = cst.alloc(8, F32, "mv")
        rstd = cst.alloc(16, F32, "rstd")
        ssq = cst.alloc(16, F32, "ssq")
        dec = cst.alloc(16, F32, "dec")
        Dacc = cst.alloc(16, F32, "Dacc")
        dprime = cst.alloc(16, F32, "dprime")
        wal = cst.alloc(8 * 16 * 2, BF16, "wal", (8, 16))

        xT = carve(OFF_R1, 32768, BF16, "xT", (8, NTOK))
        xTb = [Buf("xT%d" % i) for i in range(NT)]
        arena = carve(OFF_R1 + 32768, 65536, BF16, "arena", (8, 4096))
        arb = {(kc, s): Buf("ar%d_%d" % (kc, s)) for kc in range(8) for s in range(4)}
        alT = carve(OFF_R1 + 98304, 8192, F32, "alT", parts=16)
        alTb = [Buf("alT%d" % n) for n in range(4)]
        p1 = Bump(OFF_R1 + 106496, SB_LIMIT)

        def pe_mm(out_ap, bank_i, lhsT, rhs, reads, start=True, stop=True):
            P.op("pe", lambda e: e.matmul(out_ap, lhsT, rhs, start=start, stop=stop), reads, [bankb[bank_i]])

        def pe_tr(out_ap, bank_i, in_ap, reads):
            P.op("pe", lambda e: e.transpose(out_ap, in_ap, ident.ap), list(reads) + [ident.b], [bankb[bank_i]])

        def act(out_ap, in_ap, func, reads, writes, bias=None, scale=1.0, accum=None):
            kw = {}
            if bias is not None:
                kw["bias"] = bias
            if accum is not None:
                kw["accum_out"] = accum
            P.op("act", lambda e: e.activation(out=out_ap, in_=in_ap, func=func, scale=scale, **kw), reads, writes)

        def tt(eng, out_ap, in0, in1, op, reads, writes):
            P.op(eng, lambda e: e.tensor_tensor(out=out_ap, in0=in0, in1=in1, op=op), reads, writes)

        def ts(eng, out_ap, in0, s1, s2, op0, op1, reads, writes):
            if op1 is None:
                P.op(eng, lambda e: e.tensor_scalar(out=out_ap, in0=in0, scalar1=s1, scalar2=None, op0=op0), reads, writes)
            else:
                P.op(eng, lambda e: e.tensor_scalar(out=out_ap, in0=in0, scalar1=s1, scalar2=s2, op0=op0, op1=op1), reads, writes)

        def stt(out_ap, in0, scalar, in1, op0, op1, reads, writes):
            P.op("dve", lambda e: e.scalar_tensor_tensor(out=out_ap, in0=in0, scalar=scalar, in1=in1, op0=op0, op1=op1), reads, writes)

        def cp(eng, out_ap, in_ap, reads, writes):
            if eng == "act":
                P.op("act", lambda e: e.copy(out=out_ap, in_=in_ap), reads, writes)
            else:
                P.op(eng, lambda e: e.tensor_copy(out=out_ap, in_=in_ap), reads, writes)

        def dma(eng, out_ap, in_ap, reads, writes):
            P.dma(eng, lambda e: e.dma_start(out=out_ap, in_=in_ap), reads, writes)

        def memset(eng, t, val):
            P.op(eng, lambda e: e.memset(t.ap, val), [], [t.b])

        def bk(i, a=0, b=512):
            return banks[i][:, a:b]

        def bk3(i, nb=4):
            return banks[i][:, :].rearrange("p (a b) -> p a b", a=nb)

        xs = [carve(SB_LIMIT - 8192 + i * 4096, 4096, F32, "xs%d" % i) for i in range(2)]
        for i in range(2):
            dma("sp", xs[i].ap, x[i * 128:(i + 1) * 128, :], [], [xs[i].b])
        memset("pool", c_one, 1.0)
        memset("pool", c_lns, math.log(Q_SCALE))
        memset("pool", c_mh, -0.5)
        memset("pool", c_zero, 0.0)
        memset("pool", c_eps, LN_EPS)
        memset("pool", ident, 1.0)
        P.op("pool", lambda e: e.affine_select(out=ident.ap, in_=ident.ap, pattern=[[-1, 128]], compare_op=ALU.is_equal,
                                               fill=0.0, base=0, channel_multiplier=1), [ident.b], [ident.b])
        dma("sp", cmask.ap, cmask_d[:, :], [], [cmask.b])
        dma("sp", balow.ap[0:16, :], b_in[0, C_AL:C_AL + 16].rearrange("(p o) -> p o", o=1), [], [balow.b])
        dma("pool", wal.ap, w_in[:, C_AL:C_AL + 16].rearrange("(k p) c -> p k c", p=128), [], [wal.b])

        rows = p1.alloc(512, F32, "rows")
        memset("pool", rows, 0.0)
        dma("sp", rows.ap[0:40, :], b_in[0, 0:C_AL].rearrange("(j p) -> j p", p=128), [], [rows.b])
        dma("sp", rows.ap[40:56, :], b_in[0, C_GA:D_IN].rearrange("(j p) -> j p", p=128), [], [rows.b])
        dma("sp", rows.ap[56:64, :], sg_ln_g[0, :].rearrange("(j p) -> j p", p=128), [], [rows.b])
        dma("sp", rows.ap[64:72, :], sg_ln_b[0, :].rearrange("(j p) -> j p", p=128), [], [rows.b])
        dma("sp", rows.ap[72:80, :], gnorm_d[0, :].rearrange("(j p) -> j p", p=128), [], [rows.b])
        P.wait_all("pe", [rows.b])
        for k in range(N_DMA_SEMS):
            if P.dma_cnt[k]:
                P._wait("pe", ("dma", k, P.dma_cnt[k]))
        pe_tr(bk(0, 0, 128), 0, rows.ap, [rows.b])
        cp("dve", cols.ap, bk(0, 0, 80), [bankb[0]], [cols.b])

        def col(j):
            return cols.ap[:, j:j + 1]

        def load_arena(dst, dstb, src, c0, ncols, a0):
            nkc = dst.ap.shape[1]
            for kc in range(nkc):
                o = 0
                while o < ncols:
                    a = a0 + o
                    n = min(ncols - o, 1024 - (a % 1024))
                    dma("pool", dst.ap[:, kc, a:a + n], src[kc * 128:(kc + 1) * 128, c0 + o:c0 + o + n],
                        [], [dstb[(kc, a // 1024)]])
                    o += n

        def arr(kc, a):
            return arb[(kc, a // 1024)]

        Linc = p1.alloc(512, F32, "Linc")
        Ust = p1.alloc(512, F32, "Ust")
        mask4 = p1.alloc(2048, F32, "mask4")
        wg2 = p1.alloc(2048, F32, "wg2", parts=16)
        bgate = p1.alloc(2048, F32, "bgate")
        bias_bc = p1.alloc(4096, F32, "bias_bc")
        S = p1.alloc(4096, F32, "S", (4, 256))
        S_hb = [Buf("S_h%d" % h) for h in range(4)]
        S_bf = p1.alloc(2048, BF16, "S_bf", (4, 256))
        maskT = p1.alloc(512, F32, "maskT")
        wst8 = p1.alloc(4096, F32, "wst8", (8, 128))
        p1_base = p1.cur

        memset("pool", Linc, -1.0 / GATE_TEMP)
        P.op("pool", lambda e: e.affine_select(out=Linc.ap, in_=Linc.ap, pattern=[[1, 128]], compare_op=ALU.is_ge,
                                               fill=0.0, base=0, channel_multiplier=-1), [Linc.b], [Linc.b])
        memset("pool", Ust, -1.0 / GATE_TEMP)
        P.op("pool", lambda e: e.affine_select(out=Ust.ap, in_=Ust.ap, pattern=[[-1, 128]], compare_op=ALU.is_gt,
                                               fill=0.0, base=0, channel_multiplier=1), [Ust.b], [Ust.b])
        memset("pool", mask4, 1.0)
        P.op("pool", lambda e: e.affine_select(out=mask4.ap, in_=mask4.ap, pattern=[[0, 4], [1, 128]], compare_op=ALU.is_ge,
                                               fill=0.0, base=0, channel_multiplier=-1), [mask4.b], [mask4.b])
        memset("pool", maskT, 1.0)
        P.op("pool", lambda e: e.affine_select(out=maskT.ap, in_=maskT.ap, pattern=[[-1, 128]], compare_op=ALU.is_ge,
                                               fill=0.0, base=0, channel_multiplier=1), [maskT.b], [maskT.b])
        dma("sp", wg2.ap, wg2_d[:, :], [], [wg2.b])
        dma("sp", bgate.ap, bgate_d[0, :].partition_broadcast(128), [], [bgate.b])

        arenaA = T(mT.ap, "arenaA")
        arbA = {(kc, sg_): Buf("arA%d_%d" % (kc, sg_)) for kc in range(8) for sg_ in range(2)}
        load_arena(arenaA, arbA, w_in, C_K, 512, 512)
        load_arena(arenaA, arbA, w_in, C_V, 1024, 1024)
        xT4 = xT.ap

        def alow_group(n):
            for kc in range(8):
                pe_mm(banks[4][0:16, :], 4, wal.ap[:, kc, :], xT4[:, kc, n * 512:(n + 1) * 512],
                      [wal.b] + xTb[n * 4:(n + 1) * 4], start=(kc == 0), stop=(kc == 7))
            act(alT.ap[:, n * 512:(n + 1) * 512], banks[4][0:16, :], AF.Identity, [bankb[4], balow.b], [alTb[n]],
                bias=balow.ap[0:16, :])

        for i in range(NT):
            s = xs[i % 2]
            if i >= 2:
                dma("sp", s.ap, x[i * 128:(i + 1) * 128, :], [], [s.b])
            if i % 4 == 0 and i >= 4:
                alow_group(i // 4 - 1)
            b0 = (i % 2) * 2
            for j in range(8):
                pe_tr(bk(b0 + j // 4, (j % 4) * 128, (j % 4 + 1) * 128), b0 + j // 4, s.ap[:, j * 128:(j + 1) * 128], [s.b])
            cp("act", xT4[:, 0:4, i * 128:(i + 1) * 128], bk3(b0), [bankb[b0]], [xTb[i]])
            cp("dve", xT4[:, 4:8, i * 128:(i + 1) * 128], bk3(b0 + 1), [bankb[b0 + 1]], [xTb[i]])

        gate_b = Buf("p0_done")
        P.op("pool", lambda e: e.memset(c_zero.ap, 0.0), [xTb[NT - 3]], [c_zero.b, gate_b])
        def load_after(dst, dstb, src, c0, ncols, a0):
            nkc = dst.ap.shape[1]
            for kc in range(nkc):
                dma("pool", dst.ap[:, kc, a0:a0 + ncols], src[kc * 128:(kc + 1) * 128, c0:c0 + ncols],
                    [gate_b], [dstb[(kc, a0 // 1024)]])
        wst8g = [Buf("wst8_%d" % g) for g in range(8)]
        for g in range(8):
            dma("sp", wst8.ap[:, g, :], sg_w_s[g * 128:(g + 1) * 128, :], [gate_b], [wst8g[g]])
        load_after(arena, arb, w_in, C_U, 1024, 0)
        load_after(arena, arb, w_in, C_GA, 1024, 2048)
        load_after(arena, arb, w_in, C_VS, 1024, 1024)
        load_after(arena, arb, w_in, C_GB, 1024, 3072)
        alow_group(3)

        dec2 = [cst.alloc(16, F32, "dec%d" % i) for i in range(2)]

        def gla_pass(full, ar, arbufs):
            p1.cur = p1_base
            zb = [p1.alloc(2048, F32, "zb%d" % i) for i in range(2)]
            lb = p1.alloc(2048, F32, "lb")
            Erev = p1.alloc(2048, F32, "Erev")
            kdec = [p1.alloc(1024, BF16, "kdec%d" % i) for i in range(2)]
            vsb = [p1.alloc(2048, BF16, "vsb%d" % i) for i in range(2)]
            kT = p1.alloc(4096, F32, "kT", (4, ST))
            kT_hb = [Buf("kT_h%d" % h) for h in range(4)]
            qT_hb = [Buf("qT_h%d" % h) for h in range(4)]
            temps = zb + [lb, Erev] + kdec + vsb
            if full:
                EkT = p1.alloc(2048, F32, "EkT")
                EqT = p1.alloc(2048, F32, "EqT")
                qpT = [p1.alloc(1024, BF16, "qpT%d" % i, (4, 128)) for i in range(2)]
                kpT = [p1.alloc(1024, BF16, "kpT%d" % i, (4, 128)) for i in range(2)]
                scm = p1.alloc(1024, BF16, "scm")
                on = p1.alloc(4096, F32, "on")
                on_hb = [Buf("on_h%d" % h) for h in range(4)]
                ssq_hb = [Buf("ssq_h%d" % h) for h in range(4)]
                qT = p1.alloc(4096, F32, "qT", (4, ST))
                gbT = p1.alloc(8192, F32, "gbT", (8, ST))
                gbT_vb = [Buf("gbT_v%d" % v) for v in range(8)]
                sgt = [p1.alloc(1024, F32, "sgt%d" % i) for i in range(2)]
                temps += [EkT, EqT, scm] + qpT + kpT + sgt
            P.fence([t.b for t in temps] + (on_hb + ssq_hb + gbT_vb if full else []) + kT_hb + (qT_hb if full else []))
            dma("sp", bias_bc.ap, b_in[0, C_V:C_V + 1024].partition_broadcast(128), [], [bias_bc.b])
            pb = [0]

            def arr_(kc, a):
                return arbufs[(kc, a // 1024)]

            def proj_fm(a0, t0, evac):
                bi = pb[0] % 2
                pb[0] += 1
                for kc in range(8):
                    pe_mm(bk(bi, 0, ST), bi, ar.ap[:, kc, a0:a0 + 128], xT4[:, kc, t0:t0 + ST],
                          [arr_(kc, a0)] + xTb[t0 // 128:t0 // 128 + TPS], start=(kc == 0), stop=(kc == 7))
                evac(bi)

            def proj_a(st):
                t0 = st * ST
                for hb in range(4):
                    proj_fm(512 + hb * 128, t0,
                            lambda bi, hb=hb: ts("dve", kT.ap[:, hb, :], bk(bi, 0, ST), col(bcol(C_K + hb * 128)), None, ALU.add, None,
                                                 [bankb[bi], cols.b], [kT_hb[hb]]))
                if full:
                    for hb in range(4):
                        proj_fm(hb * 128, t0,
                                lambda bi, hb=hb: ts("dve", qT.ap[:, hb, :], bk(bi, 0, ST), col(bcol(C_Q + hb * 128)), None, ALU.add, None,
                                                     [bankb[bi], cols.b], [qT_hb[hb]]))

            def proj_b(st):
                t0 = st * ST
                for vb in range(8):
                    proj_fm(2048 + vb * 128, t0,
                            lambda bi, vb=vb: act(gbT.ap[:, vb, :], bk(bi, 0, ST), AF.Silu, [bankb[bi], cols.b], [gbT_vb[vb]],
                                                  bias=col(bcol(C_R + vb * 128))))
                for vb in range(8):
                    sg = sgt[vb % 2]

                    def ev(bi, vb=vb, sg=sg):
                        act(sg.ap, bk(bi, 0, ST), AF.Sigmoid, [bankb[bi], cols.b], [sg.b], bias=col(bcol(C_GB + vb * 128)))
                        stt(gbT.ap[:, vb, :], gbT.ap[:, vb, :], col(COL_GN + vb), sg.ap, ALU.mult, ALU.mult,
                            [gbT_vb[vb], sg.b, cols.b], [gbT_vb[vb]])
                    proj_fm(3072 + vb * 128, t0, ev)

            def early(i):
                p = i % 2
                j = i % TPS
                tc0, tc1 = j * 128, (j + 1) * 128
                z, dc, kd, vs_ = zb[p], dec2[p], kdec[p], vsb[p]

                def e0():
                    pe_mm(bk(2), 2, alT.ap[:, i * 128:(i + 1) * 128], wg2.ap, [alTb[i // 4], wg2.b])
                    tt("dve", z.ap, bk(2), bgate.ap, ALU.add, [bankb[2], bgate.b], [z.b])
                    act(lb.ap, z.ap, AF.Exp, [z.b], [lb.b], scale=-1.0)
                    act(z.ap, lb.ap, AF.Ln, [lb.b, c_one.b], [z.b], bias=c_one.ap)

                def e1():
                    for h in range(4):
                        pe_mm(bk(3, h * 128, (h + 1) * 128), 3, z.ap[:, h * 128:(h + 1) * 128], Linc.ap, [z.b, Linc.b])
                    pe_mm(bk(4), 4, Ust.ap, z.ap, [Ust.b, z.b])
                    act(dc.ap, bk3(3)[:, :, 127], AF.Exp, [bankb[3]], [dc.b])
                    if full:
                        act(EkT.ap, bk(3), AF.Exp, [bankb[3]], [EkT.b], scale=-1.0)
                        act(EqT.ap, bk(3), AF.Exp, [bankb[3], c_lns.b], [EqT.b], bias=c_lns.ap)
                    act(Erev.ap, bk(4), AF.Exp, [bankb[4]], [Erev.b])
                    if full:
                        tt("dve", kpT[p].ap, kT.ap[:, :, tc0:tc1], EkT.ap.rearrange("p (a b) -> p a b", a=4), ALU.mult,
                           kT_hb + [EkT.b], [kpT[p].b])
                        tt("dve", qpT[p].ap, qT.ap[:, :, tc0:tc1], EqT.ap.rearrange("p (a b) -> p a b", a=4), ALU.mult,
                           qT_hb + [EqT.b], [qpT[p].b])

                def e2():
                    for h in range(4):
                        pe_tr(bk(5, h * 128, (h + 1) * 128), 5, kT.ap[:, h, tc0:tc1], [kT_hb[h]])
                    tt("dve", kd.ap, bk(5), Erev.ap, ALU.mult, [bankb[5], Erev.b], [kd.b])

                def e3(halves=(0, 1)):
                    for half in halves:
                        for kc in range(8):
                            pe_mm(bk(6 + half), 6 + half, xT4[:, kc, i * 128:(i + 1) * 128],
                                  ar.ap[:, kc, 1024 + half * 512:1024 + (half + 1) * 512],
                                  [xTb[i], arr_(kc, 1024)], start=(kc == 0), stop=(kc == 7))
                        tt("dve", vs_.ap[:, half * 512:(half + 1) * 512], bk(6 + half), bias_bc.ap[:, half * 512:(half + 1) * 512],
                           ALU.add, [bankb[6 + half], bias_bc.b], [vs_.b])
                return [e0, e1, e2, e3]

            def late(i):
                p = i % 2
                j = i % TPS
                tc0, tc1 = j * 128, (j + 1) * 128
                dc, kd, vs_ = dec2[p], kdec[p], vsb[p]
                OB = (1, 2)
                TB = (3, 4)
                UB = (5, 0)

                def l0():
                    if not full:
                        return
                    for h in range(4):
                        pe_mm(bk(0, h * 128, (h + 1) * 128), 0, kpT[p].ap[:, h, :], qpT[p].ap[:, h, :], [kpT[p].b, qpT[p].b])
                    tt("dve", scm.ap, bk(0), mask4.ap, ALU.mult, [bankb[0], mask4.b], [scm.b])

                def l1():
                    if not full:
                        return
                    for h in range(4):
                        ob, oc = OB[h // 2], (h % 2) * 256
                        pe_mm(bk(ob, oc, oc + 256), ob, scm.ap[:, h * 128:(h + 1) * 128], vs_.ap[:, h * 256:(h + 1) * 256],
                              [scm.b, vs_.b], start=True, stop=False)
                        pe_mm(bk(ob, oc, oc + 256), ob, qpT[p].ap[:, h, :], S_bf.ap[:, h, :], [qpT[p].b, S_bf.b],
                              start=False, stop=True)
                    for h in range(4):
                        ob, oc = OB[h // 2], (h % 2) * 256
                        act(on.ap[:, h * 256:(h + 1) * 256], bk(ob, oc, oc + 256), AF.Square, [bankb[ob]], [on_hb[h], ssq_hb[h]],
                            accum=ssq.ap[:, h:h + 1])
                    act(rstd.ap, ssq.ap, AF.Ln, ssq_hb + [c_eps.b], [rstd.b], bias=c_eps.ap, scale=1.0 / 256.0)
                    act(rstd.ap, rstd.ap, AF.Exp, [rstd.b], [rstd.b], scale=-0.5)
                    for h in range(4):
                        ob, oc = OB[h // 2], (h % 2) * 256
                        act(on.ap[:, h * 256:(h + 1) * 256], bk(ob, oc, oc + 256), AF.Identity, [bankb[ob], rstd.b], [on_hb[h]],
                            scale=rstd.ap[:, h:h + 1])

                def l2():
                    if not full:
                        return
                    for vb in range(8):
                        tb = TB[vb // 4]
                        pe_tr(bk(tb, (vb % 4) * 128, (vb % 4 + 1) * 128), tb, on.ap[:, vb * 128:(vb + 1) * 128], [on_hb[vb // 2]])
                    on3 = on.ap.rearrange("p (a b) -> p a b", a=8)
                    for hf in range(2):
                        tb = TB[hf]
                        tt("dve", on3[:, hf * 4:(hf + 1) * 4, :], bk3(tb), gbT.ap[:, hf * 4:(hf + 1) * 4, tc0:tc1], ALU.mult,
                           [bankb[tb]] + gbT_vb[hf * 4:(hf + 1) * 4], [on_hb[2 * hf], on_hb[2 * hf + 1]])
                    msl = mT.ap[:, :, i * 128:(i + 1) * 128]
                    tt("dve", msl, on3, msl, ALU.add, on_hb + [mTb[i]], [mTb[i]])

                def l3():
                    for h in range(4):
                        ub, uc = UB[h // 2], (h % 2) * 256
                        pe_mm(bk(ub, uc, uc + 256), ub, kd.ap[:, h * 128:(h + 1) * 128], vs_.ap[:, h * 256:(h + 1) * 256],
                              [kd.b, vs_.b])
                    for h in range(4):
                        ub, uc = UB[h // 2], (h % 2) * 256
                        stt(S.ap[:, h, :], S.ap[:, h, :], dc.ap[:, h:h + 1], bk(ub, uc, uc + 256), ALU.mult, ALU.add,
                            [S_hb[h], dc.b, bankb[ub]], [S_hb[h]])
                    if full:
                        cp("dve", S_bf.ap, S.ap, S_hb, [S_bf.b])
                    else:
                        tt("dve", Dacc.ap, Dacc.ap, dc.ap, ALU.mult, [Dacc.b, dc.b], [Dacc.b])
                return [l0, l1, l2, l3]

            for i in range(NT + 1):
                if i < NT and i % TPS == 0:
                    proj_a(i // TPS)
                E = early(i) if i < NT else [lambda *a: None] * 4
                L = late(i - 1) if i >= 1 else [lambda *a: None] * 4
                E[0]()
                L[0]()
                E[3]((0,))
                E[1]()
                L[1]()
                E[3]((1,))
                E[2]()
                L[3]()
                L[2]()
                if full and i < NT and i % TPS == 0:
                    proj_b(i // TPS)

        P.op("dve", lambda e: e.memset(S.ap, 0.0), [], S_hb)
        memset("dve", Dacc, 1.0)
        gla_pass(False, arenaA, arbA)
        p1.cur = p1_base
        pay = p1.alloc(4128, F32, "pay")
        P.fence([pay.b])
        cp("dve", pay.ap[:, 0:1024], S.ap.rearrange("p a b -> p (a b)"), S_hb, [pay.b])
        cp("dve", pay.ap[:, 1024:1028], Dacc.ap, [Dacc.b, pay.b], [pay.b])
        bcc_in, bcc_out = Buf("cc_in"), Buf("cc_out")
        dma("sp", cc_in.ap()[:, :], pay.ap[:, 0:1028], [pay.b], [bcc_in])
        cc_tok = P.ext("pool", lambda e: e.collective_compute("AllGather", ALU.bypass, replica_groups=[[0, 1, 2, 3], [4, 5, 6, 7]],
                                                              ins=[cc_in.ap().opt()], outs=[cc_out.ap().opt()]),
                       [bcc_in], [bcc_out])
        P._wait("pool", cc_tok)
        p1.cur = p1_base
        wsT = p1.alloc(2048, BF16, "wsT", (8, 128))
        Cg = p1.alloc(4096, F32, "Cg", (8, 128))
        wst = p1.alloc(512, F32, "wst")
        wsf = p1.alloc(512, F32, "wsf")
        onesf = p1.alloc(512, F32, "onesf")
        ugT2 = [p1.alloc(8192, F32, "ugT%d" % i, (8, ST)) for i in range(2)]
        ugT_cb = [[Buf("ugT%d_c%d" % (i, c)) for c in range(8)] for i in range(2)]
        sgb1 = [p1.alloc(1024, F32, "sgb1_%d" % i) for i in range(2)]
        zt2 = [p1.alloc(4096, F32, "zt%d" % i) for i in range(2)]
        vn2 = [p1.alloc(2048, BF16, "vn%d" % i) for i in range(2)]
        yat2 = [p1.alloc(4096, F32, "yat%d" % i, (8, 128)) for i in range(2)]
        yat_gb = [[Buf("yat%d_g%d" % (i, g)) for g in range(8)] for i in range(2)]
        P.fence([wsT.b, Cg.b, wst.b, wsf.b, onesf.b, sgb1[0].b, sgb1[1].b]
                + [t.b for t in zt2 + vn2 + yat2] + [b_ for l_ in ugT_cb for b_ in l_])
        P.fence(mTb)
        dma("sp", bias_bc.ap, b_in[0, C_VS:C_VS + 1024].partition_broadcast(128), [], [bias_bc.b])
        dma("sp", Cg.ap.rearrange("p a b -> p (a b)"), sg_b_s[0, :].partition_broadcast(128), [], [Cg.b])
        memset("dve", onesf, 1.0)
        wsf8 = T(yat2[1].ap, "wsf8")
        P.fence([wsf8.b])
        for g in range(8):
            tt("dve", wst8.ap[:, g, :], wst8.ap[:, g, :], maskT.ap, ALU.mult, [wst8.b, wst8g[g], maskT.b], [wst8.b])
        for g in range(8):
            tb = 2 + g // 4
            pe_tr(bk(tb, (g % 4) * 128, (g % 4 + 1) * 128), tb, wst8.ap[:, g, :], [wst8.b])
        for hf in range(2):
            cp("act", wsf8.ap[:, hf * 4:(hf + 1) * 4, :], bk3(2 + hf), [bankb[2 + hf]], [wsf8.b])
            cp("dve", wsT.ap[:, hf * 4:(hf + 1) * 4, :], bk3(2 + hf), [bankb[2 + hf]], [wsT.b])
        for g in range(8):
            tb = 4 + g // 4
            pe_mm(bk(tb, (g % 4) * 128, (g % 4 + 1) * 128), tb, onesf.ap, wsf8.ap[:, g, :], [onesf.b, wsf8.b])
        for g in range(8):
            tb = 4 + g // 4
            stt(Cg.ap[:, g, :], bk(tb, (g % 4) * 128, (g % 4 + 1) * 128), col(COL_SGB + g), Cg.ap[:, g, :], ALU.mult, ALU.add,
                [bankb[tb], cols.b, Cg.b], [Cg.b])
        P.fence([b_ for l_ in yat_gb for b_ in l_])
        pb1 = [0]

        def b1_stproj(st, part):
            t0 = st * ST
            ugT = ugT2[st % 2]
            for cb in (range(8) if part == 0 else []):
                bi = pb1[0] % 2
                pb1[0] += 1
                for kc in range(8):
                    pe_mm(bk(bi, 0, ST), bi, arena.ap[:, kc, cb * 128:(cb + 1) * 128], xT4[:, kc, t0:t0 + ST],
                          [arr(kc, cb * 128)] + xTb[t0 // 128:t0 // 128 + TPS], start=(kc == 0), stop=(kc == 7))
                act(ugT.ap[:, cb, :], bk(bi, 0, ST), AF.Gelu_apprx_tanh, [bankb[bi], cols.b], [ugT_cb[st % 2][cb]], bias=col(bcol(C_U + cb * 128)))
            for cb in (range(8) if part == 1 else []):
                bi = pb1[0] % 2
                pb1[0] += 1
                for kc in range(8):
                    pe_mm(bk(bi, 0, ST), bi, arena.ap[:, kc, 2048 + cb * 128:2048 + (cb + 1) * 128], xT4[:, kc, t0:t0 + ST],
                          [arr(kc, 2048)] + xTb[t0 // 128:t0 // 128 + TPS], start=(kc == 0), stop=(kc == 7))
                sg = sgb1[cb % 2]
                act(sg.ap, bk(bi, 0, ST), AF.Sigmoid, [bankb[bi], cols.b], [sg.b], bias=col(bcol(C_GA + cb * 128)))
                tt("dve", ugT.ap[:, cb, :], ugT.ap[:, cb, :], sg.ap, ALU.mult, [ugT_cb[st % 2][cb], sg.b], [ugT_cb[st % 2][cb]])

        def b1_vs(i):
            zt, vn = zt2[i % 2], vn2[i % 2]
            vb0 = 2 if i % 2 == 0 else 6
            for half in range(2):
                for kc in range(8):
                    pe_mm(bk(vb0 + half), vb0 + half, xT4[:, kc, i * 128:(i + 1) * 128],
                          arena.ap[:, kc, 1024 + half * 512:1024 + (half + 1) * 512],
                          [xTb[i], arr(kc, 1024)], start=(kc == 0), stop=(kc == 7))
                tt("dve", zt.ap[:, half * 512:(half + 1) * 512], bk(vb0 + half), bias_bc.ap[:, half * 512:(half + 1) * 512],
                   ALU.add, [bankb[vb0 + half], bias_bc.b], [zt.b])
            act(zt.ap, zt.ap, AF.Gelu_apprx_tanh, [zt.b], [zt.b])
            for half in range(2):
                P.op("dve", lambda e, half=half: e.bn_stats(out=st6.ap[:, half * 6:(half + 1) * 6],
                                                           in_=zt.ap[:, half * 512:(half + 1) * 512]), [zt.b], [st6_hb[half]])
            P.op("dve", lambda e: e.bn_aggr(out=mv.ap, in_=st6.ap), st6_hb, [mv.b])
            if i >= NT - 3:
                act(rstd.ap[:, 0:1], mv.ap[:, 1:2], AF.Ln, [mv.b, c_eps.b], [rstd.b], bias=c_eps.ap)
                act(rstd.ap[:, 0:1], rstd.ap[:, 0:1], AF.Exp, [rstd.b], [rstd.b], scale=-0.5)
            else:
                ts("dve", rstd.ap[:, 0:1], mv.ap[:, 1:2], LN_EPS, None, ALU.add, None, [mv.b], [rstd.b])
                tt("pool", rstd.ap[:, 0:1], rstd.ap[:, 0:1], c_mh.ap[:, 0:1], ALU.pow, [rstd.b, c_mh.b], [rstd.b])
            ts("dve", vn.ap, zt.ap, mv.ap[:, 0:1], rstd.ap[:, 0:1], ALU.subtract, ALU.mult, [zt.b, mv.b, rstd.b], [vn.b])

        def b1_mix(i):
            vn, yat, ugT = vn2[i % 2], yat2[i % 2], ugT2[(i // TPS) % 2]
            j = i % TPS
            for g in range(8):
                mb = 4 + g // 4
                pe_mm(bk(mb, (g % 4) * 128, (g % 4 + 1) * 128), mb, vn.ap[:, g * 128:(g + 1) * 128], wsT.ap[:, g, :],
                      [vn.b, wsT.b])
            for g in range(8):
                mb = 4 + g // 4
                stt(yat.ap[:, g, :], bk(mb, (g % 4) * 128, (g % 4 + 1) * 128), col(COL_SGG + g), Cg.ap[:, g, :],
                    ALU.mult, ALU.add, [bankb[mb], cols.b, Cg.b], [yat_gb[i % 2][g]])
            tt("dve", mT.ap[:, :, i * 128:(i + 1) * 128], yat.ap, ugT.ap[:, :, j * 128:(j + 1) * 128], ALU.mult,
               yat_gb[i % 2] + ugT_cb[(i // TPS) % 2], [mTb[i]])

        b1_stproj(0, 0)
        b1_stproj(0, 1)
        b1_vs(0)
        for i in range(NT):
            if i // TPS + 1 < NST:
                b1_stproj(i // TPS + 1, i % TPS)
            b1_mix(i)
            if i + 1 < NT:
                b1_vs(i + 1)
            if i == NT - 3:
                load_arena(arena, arb, w_in, C_K, 512, 512)
                load_arena(arena, arb, w_in, C_Q, 512, 0)
                load_arena(arena, arb, w_in, C_R, 1024, 2048)
            if i == NT - 2:
                load_arena(arena, arb, w_in, C_V, 1024, 1024)

        p1.cur = p1_base
        gst3 = [p1.alloc(4128, F32, "gst%d" % j) for j in range(3)]
        mS = p1.alloc(4096, F32, "mS", (4, 256))
        Sin = p1.alloc(4096, F32, "Sin", (4, 256))
        P.fence([t.b for t in gst3] + [mS.b, Sin.b])
        for jq in range(3):
            dma("sp", gst3[jq].ap[:, 0:1028], cc_out.ap()[jq * 128:(jq + 1) * 128, :], [bcc_out], [gst3[jq].b])
        memset("dve", Sin, 0.0)
        for jq in range(3):
            gst = gst3[jq]
            ts("dve", dprime.ap, gst.ap[:, 1024:1028], cmask.ap[:, jq:jq + 1], cmask.ap[:, 4 + jq:5 + jq], ALU.mult, ALU.add,
               [gst.b, cmask.b], [dprime.b])
            ts("dve", mS.ap.rearrange("p a b -> p (a b)"), gst.ap[:, 0:1024], cmask.ap[:, jq:jq + 1], None, ALU.mult, None,
               [gst.b, cmask.b], [mS.b])
            for h in range(4):
                stt(Sin.ap[:, h, :], Sin.ap[:, h, :], dprime.ap[:, h:h + 1], mS.ap[:, h, :], ALU.mult, ALU.add,
                    [Sin.b, dprime.b, mS.b], [Sin.b])
        cp("dve", S.ap, Sin.ap, [Sin.b], S_hb)
        cp("dve", S_bf.ap, Sin.ap, [Sin.b], [S_bf.b])

        p1.cur = p1_base
        gla_pass(True, arena, arb)

        if debug:
            dbgb = Buf("dbg")
            for kc in range(8):
                dma("pool", dbg[:, kc * NTOK:(kc + 1) * NTOK], mT.ap[:, kc, :], mTb, [dbgb])

        W1 = carve(OFF_R1, 65536, BF16, "W1", (8, 4096))
        W2 = carve(OFF_R1 + 65536, 65536, BF16, "W2", (32, 1024))
        Wo = carve(OFF_R1 + 131072, 16384, BF16, "Wo", (8, 1024))
        w1b = {(kc, s): Buf("w1_%d_%d" % (kc, s)) for kc in range(8) for s in range(4)}
        w2b = {(kc, 0): Buf("w2_%d" % kc) for kc in range(32)}
        wob = {(kc, 0): Buf("wo_%d" % kc) for kc in range(8)}
        p2 = Bump(OFF_W, SB_LIMIT)
        xh = [p2.alloc(4096, F32, "xh%d" % i) for i in range(4)]
        rt = [p2.alloc(1024, F32, "rt%d" % i) for i in range(3)]
        lnA = p2.alloc(4096, F32, "lnA")
        lnB = p2.alloc(4096, F32, "lnB")
        aT = [cst.alloc(512, BF16, "aT%d" % i) for i in range(3)]
        P.fence(list(w1b.values()) + list(w2b.values()) + list(wob.values()) + [t.b for t in xh] + [lnA.b, lnB.b]
                + [t.b for t in rt])
        for i in range(4):
            dma("sp", xh[i].ap, x[i * 128:(i + 1) * 128, :], [], [xh[i].b])
        ln_cur = [ln1_g]
        dma("sp", lnA.ap, ln1_g[0, :].partition_broadcast(128), [], [lnA.b])
        dma("sp", lnB.ap, ln1_b[0, :].partition_broadcast(128), [], [lnB.b])
        xdep = [t.b for t in xh] + [lnA.b, lnB.b]
        for kc in range(8):
            dma("pool", Wo.ap[:, kc, :], w_out[kc * 128:(kc + 1) * 128, :], xdep, [wob[(kc, 0)]])
        for seg in range(4):
            for kc in range(8):
                dma("pool", W1.ap[:, kc, seg * 1024:(seg + 1) * 1024], w_ff1[kc * 128:(kc + 1) * 128, seg * 1024:(seg + 1) * 1024],
                    [], [w1b[(kc, seg)]])
            for kc in range(seg * 8, (seg + 1) * 8):
                dma("pool", W2.ap[:, kc, :], w_ff2[kc * 128:(kc + 1) * 128, :], [], [w2b[(kc, 0)]])
        outb = Buf("out")

        def ln_load(g_d, b_d):
            if ln_cur[0] is not g_d:
                dma("sp", lnA.ap, g_d[0, :].partition_broadcast(128), [], [lnA.b])
                dma("sp", lnB.ap, b_d[0, :].partition_broadcast(128), [], [lnB.b])
                ln_cur[0] = g_d

        def layer_norm(xt, g_d, b_d):
            for half in range(2):
                P.op("dve", lambda e, half=half: e.bn_stats(out=st6.ap[:, half * 6:(half + 1) * 6],
                                                           in_=xt.ap[:, half * 512:(half + 1) * 512]), [xt.b], [st6_hb[half]])
            P.op("dve", lambda e: e.bn_aggr(out=mv.ap, in_=st6.ap), st6_hb, [mv.b])
            act(rstd.ap[:, 0:1], mv.ap[:, 1:2], AF.Ln, [mv.b, c_eps.b], [rstd.b], bias=c_eps.ap)
            act(rstd.ap[:, 0:1], rstd.ap[:, 0:1], AF.Exp, [rstd.b], [rstd.b], scale=-0.5)
            ts("dve", xt.ap, xt.ap, mv.ap[:, 0:1], rstd.ap[:, 0:1], ALU.subtract, ALU.mult, [xt.b, mv.b, rstd.b], [xt.b])
            ln_load(g_d, b_d)
            tt("dve", xt.ap, xt.ap, lnA.ap, ALU.mult, [xt.b, lnA.b], [xt.b])
            tt("dve", xt.ap, xt.ap, lnB.ap, ALU.add, [xt.b, lnB.b], [xt.b])

        def pre_chunks(st):
            chunks = []
            for j in range(TPS):
                i = st * TPS + j
                xt = xh[i % 4]

                def c_op(half, i=i, xt=xt):
                    if half == 0 and i >= 4:
                        dma("sp", xt.ap, x[i * 128:(i + 1) * 128, :], [], [xt.b])
                    for kc in range(8):
                        pe_mm(bk(7), 7, mT.ap[:, kc, i * 128:(i + 1) * 128], Wo.ap[:, kc, half * 512:(half + 1) * 512],
                              [mTb[i], wob[(kc, 0)]], start=(kc == 0), stop=(kc == 7))
                    sl = xt.ap[:, half * 512:(half + 1) * 512]
                    stt(sl, sl, ALPHA, bk(7), ALU.mult, ALU.add, [xt.b, bankb[7]], [xt.b])
                    if half == 1:
                        layer_norm(xt, ln1_g, ln1_b)

                def c_tr(hf, i=i, xt=xt):
                    for db in range(hf * 4, (hf + 1) * 4):
                        pe_tr(bk(7, (db % 4) * 128, (db % 4 + 1) * 128), 7, xt.ap[:, db * 128:(db + 1) * 128], [xt.b])
                    cp("act", mT.ap[:, hf * 4:(hf + 1) * 4, i * 128:(i + 1) * 128], bk3(7), [bankb[7]], [mTb[i]])

                chunks.append((j * 10 + 8, lambda f=c_op: f(0)))
                chunks.append((j * 10 + 9, lambda f=c_op: f(1)))
                chunks.append((j * 10 + 16, lambda f=c_tr: f(0)))
                chunks.append((j * 10 + 17, lambda f=c_tr: f(1)))
            return chunks

        def finish_tile(i):
            xt = xh[i % 4]
            layer_norm(xt, ln2_g, ln2_b)
            dma("sp", out[i * 128:(i + 1) * 128, :], xt.ap, [xt.b], [outb])

        for _, f in pre_chunks(0):
            f()
        for st in range(NST):
            sched = {}
            if st >= 1:
                sched[2] = [lambda i=(st - 1) * TPS: finish_tile(i)]
                sched[4] = [lambda i=(st - 1) * TPS + 1: finish_tile(i)]
            if st + 1 < NST:
                sched[5] = [lambda: ln_load(ln1_g, ln1_b)]
                for fbk_, f in pre_chunks(st + 1):
                    sched.setdefault(fbk_, []).append(f)
            sched.setdefault(28, []).append(lambda: ln_load(ln2_g, ln2_b))
            hsl = [mTb[st * TPS + j] for j in range(TPS)]

            def ff1(fb):
                fbk = 4 + fb % 3
                for kc in range(8):
                    pe_mm(bk(fbk, 0, ST), fbk, W1.ap[:, kc, fb * 128:(fb + 1) * 128], mT.ap[:, kc, st * ST:(st + 1) * ST],
                          [w1b[(kc, fb // 8)]] + hsl, start=(kc == 0), stop=(kc == 7))
                r_ = rt[fb % 3]
                a_ = aT[fb % 3]
                act(r_.ap, bk(fbk, 0, ST), AF.Relu, [bankb[fbk]], [r_.b])
                tt("dve", a_.ap, r_.ap, r_.ap, ALU.mult, [r_.b], [a_.b])

            def ff2(fb):
                a_ = aT[fb % 3]
                for j in range(TPS):
                    for half in range(2):
                        ab = j * 2 + half
                        pe_mm(bk(ab), ab, a_.ap[:, j * 128:(j + 1) * 128], W2.ap[:, fb, half * 512:(half + 1) * 512],
                              [a_.b, w2b[(fb, 0)]], start=(fb == 0), stop=(fb == 31))

            ff1(0)
            ff1(1)
            for fb in range(32):
                if fb + 2 < 32:
                    ff1(fb + 2)
                ff2(fb)
                for f in sched.get(fb, []):
                    f()
            for j in range(TPS):
                i = st * TPS + j
                xt = xh[i % 4]
                for half in range(2):
                    sl = xt.ap[:, half * 512:(half + 1) * 512]
                    stt(sl, sl, ALPHA, bk(j * 2 + half), ALU.mult, ALU.add, [xt.b, bankb[j * 2 + half]], [xt.b])
        for j in range(TPS):
            finish_tile((NST - 1) * TPS + j)

        for k in range(N_DMA_SEMS):
            if P.dma_cnt[k]:
                P._wait("sp", ("dma", k, P.dma_cnt[k]))
        P.emit(block, esem, dsems)
    return nc


_NC_CACHE = {}


def _prep_inputs(inputs):
    f = lambda a: np.ascontiguousarray(np.asarray(a, dtype=np.float32))
    x = f(inputs["x"]).reshape(2 * 8192, D)
    shared = {
        "w_in": f(inputs["w_in"]).reshape(D, D_IN),
        "b_in": f(inputs["b_in"]).reshape(1, D_IN),
        "sg_ln_g": f(inputs["sg_ln_g"]).reshape(1, D),
        "sg_ln_b": f(inputs["sg_ln_b"]).reshape(1, D),
        "sg_w_s": f(inputs["sg_w_s"]).reshape(8 * 128, 128),
        "sg_b_s": f(inputs["sg_b_s"]).reshape(1, D),
        "gla_w_gate2": f(inputs["gla_w_gate2"]).reshape(16, 512),
        "gla_b_gate": f(inputs["gla_b_gate"]).reshape(1, 512),
        "gla_norm_g": f(inputs["gla_norm_g"]).reshape(1, D),
        "w_out": f(inputs["w_out"]).reshape(D, D),
        "ln1_g": f(inputs["ln1_g"]).reshape(1, D),
        "ln1_b": f(inputs["ln1_b"]).reshape(1, D),
        "w_ff1": f(inputs["w_ff1"]).reshape(D, D_FF),
        "w_ff2": f(inputs["w_ff2"]).reshape(D_FF, D),
        "ln2_g": f(inputs["ln2_g"]).reshape(1, D),
        "ln2_b": f(inputs["ln2_b"]).reshape(1, D),
    }
    in_maps = []
    for c in range(8):
        qc = c % 4
        cm = np.zeros((128, 8), np.float32)
        for j in range(4):
            cm[:, j] = 1.0 if j < qc else 0.0
            cm[:, 4 + j] = 0.0 if j < qc else 1.0
        m = dict(shared)
        m["x"] = np.ascontiguousarray(x[c * NTOK:(c + 1) * NTOK])
        m["cmask"] = cm
        in_maps.append(m)
    return in_maps


def kernel(**inputs):
    if "nc" not in _NC_CACHE:
        _NC_CACHE["nc"] = build_nc()
    nc = _NC_CACHE["nc"]
    in_maps = _prep_inputs(inputs)
    res = run_bass_kernel_spmd(nc, in_maps, core_ids=list(range(8)))
    outs = [np.asarray(res.results[c]["out"], dtype=np.float32) for c in range(8)]
    return np.concatenate(outs, axis=0).reshape(2, 8192, D)
```

```python
import math
from contextlib import ExitStack

import numpy as np
import concourse.bass as bass
import concourse.mybir as mybir
from concourse.bass_utils import run_bass_kernel_spmd

F32 = mybir.dt.float32
BF16 = mybir.dt.bfloat16
AF = mybir.ActivationFunctionType
ALU = mybir.AluOpType

ENGS = ["pe", "act", "dve", "pool", "sp"]
N_DMA_SEMS = 24

D = 1024
NTOK = 2048
NT = NTOK // 128
ST = 256
NST = NTOK // ST
TPS = ST // 128
D_IN = 7184
C_U, C_VS, C_Q, C_K, C_V, C_R, C_AL, C_GA, C_GB = 0, 1024, 2048, 2560, 3072, 4096, 5120, 5136, 6160
D_FF = 4096
LN_EPS = 1e-5
ALPHA = 2.0 ** 0.25
GATE_TEMP = 16.0
Q_SCALE = 128.0 ** -0.5


def bcol(c0):
    return c0 // 128 if c0 < C_AL else 40 + (c0 - C_GA) // 128


COL_SGG, COL_SGB, COL_GN = 56, 64, 72


class Buf:
    def __init__(self, name, excl=False):
        self.name = name
        self.excl = excl
        self.w = None
        self.w_read = False
        self.r = {}


class Prog:
    def __init__(self):
        self.q = {e: [] for e in ENGS}
        self.seen = {e: {} for e in ENGS}
        self.dma_cnt = [0] * (N_DMA_SEMS + 1)
        self.dma_rr = 0
        self.dma_rr_pool = 0
        self.last_op = {e: None for e in ENGS}

    def _wait(self, eng, tok):
        if tok is None:
            return
        if tok[0] == "eng":
            _, e2, idx = tok
            key = ("eng", e2)
            if self.seen[eng].get(key, -1) >= idx:
                return
            self.seen[eng][key] = idx
            self.q[e2][idx]["inc"] = True
            self.q[eng].append(dict(kind="wait", tok=tok))
        else:
            _, k, val = tok
            key = ("dma", k)
            if self.seen[eng].get(key, -1) >= val:
                return
            self.seen[eng][key] = val
            self.q[eng].append(dict(kind="wait", tok=tok))

    def _deps(self, eng, reads, writes, is_dma):
        toks = []
        writes = list(writes)
        excl_reads = []
        for b in reads:
            if b.excl:
                if b not in writes and b not in excl_reads:
                    excl_reads.append(b)
                continue
            if b.w is not None:
                toks.append(b.w)
        for b in writes:
            if b.w is not None:
                toks.append(b.w)
            toks.extend(b.r.values())
        for b in excl_reads:
            t = b.w
            if t is None:
                continue
            if t[0] == "eng" and t[1] == eng and not is_dma and b.w_read:
                continue
            toks.append(t)
        out = []
        for t in toks:
            if t[0] == "eng" and t[1] == eng and not is_dma and eng == "pe":
                continue
            out.append(t)
        return out, writes, excl_reads

    def op(self, eng, fn, reads=(), writes=()):
        reads = [b for b in reads if b is not None]
        writes = [b for b in writes if b is not None]
        toks, writes2, excl_reads = self._deps(eng, reads, writes, False)
        for t in toks:
            self._wait(eng, t)
        idx = len(self.q[eng])
        self.q[eng].append(dict(kind="op", fn=fn, inc=False))
        tok = ("eng", eng, idx)
        self.last_op[eng] = tok
        for b in writes2:
            b.w = tok
            b.w_read = False
            b.r = {}
        for b in excl_reads:
            b.w = tok
            b.w_read = True
            b.r = {}
        for b in reads:
            if b not in writes2 and b not in excl_reads:
                b.r[eng] = tok
        return tok

    def dma(self, eng, fn, reads=(), writes=()):
        reads = [b for b in reads if b is not None]
        writes = [b for b in writes if b is not None]
        toks, writes2, _er = self._deps(eng, reads, writes, True)
        assert not _er
        for t in toks:
            self._wait(eng, t)
        half = N_DMA_SEMS // 2
        if eng == "pool":
            k = half + self.dma_rr_pool
            self.dma_rr_pool = (self.dma_rr_pool + 1) % half
        else:
            k = self.dma_rr
            self.dma_rr = (self.dma_rr + 1) % half
        prev = self.dma_cnt[k]
        if prev > 0:
            self._wait(eng, ("dma", k, prev))
        self.dma_cnt[k] = prev + 16
        tok = ("dma", k, prev + 16)
        self.q[eng].append(dict(kind="dma", fn=fn, k=k))
        for b in writes2:
            b.w = tok
            b.r = {}
        for b in reads:
            if b not in writes2:
                b.r[("dma", k)] = tok
        return tok

    def ext(self, eng, fn, reads=(), writes=()):
        toks, writes2, _er = self._deps(eng, list(reads), list(writes), True)
        for t in toks:
            self._wait(eng, t)
        k = N_DMA_SEMS
        self.dma_cnt[k] += 1
        tok = ("dma", k, self.dma_cnt[k])
        self.q[eng].append(dict(kind="ext", fn=fn, k=k))
        for b in writes2:
            b.w = tok
            b.r = {}
        for b in reads:
            if b not in writes2:
                b.r[("dma", k)] = tok
        return tok

    def fence_tokens(self):
        toks = {}
        for e in ENGS:
            if self.last_op[e] is not None:
                toks[("f", e)] = self.last_op[e]
        for k in range(N_DMA_SEMS // 2):
            if self.dma_cnt[k]:
                toks[("fd", k)] = ("dma", k, self.dma_cnt[k])
        return toks

    def fence(self, bufs):
        toks = self.fence_tokens()
        for b in bufs:
            b.w = None
            b.w_read = False
            b.r = dict(toks)

    def wait_all(self, eng, bufs):
        for b in bufs:
            if b.w is not None:
                self._wait(eng, b.w)

    def emit(self, block, esem, dsems):
        cnt_at = {}
        for e in ENGS:
            c = 0
            for i, r in enumerate(self.q[e]):
                if r["kind"] == "op" and r["inc"]:
                    c += 1
                    cnt_at[(e, i)] = c
        qs = self.q

        def run(e, engine):
            for r in qs[e]:
                if r["kind"] == "wait":
                    t = r["tok"]
                    if t[0] == "eng":
                        engine.wait_ge(esem[t[1]], cnt_at[(t[1], t[2])])
                    else:
                        engine.wait_ge(dsems[t[1]], t[2])
                elif r["kind"] == "op":
                    inst = r["fn"](engine)
                    if r["inc"]:
                        inst.then_inc(esem[e], 1)
                elif r["kind"] == "ext":
                    inst = r["fn"](engine)
                    inst.then_inc(dsems[r["k"]], 1)
                else:
                    inst = r["fn"](engine)
                    inst.then_inc(dsems[r["k"]], 16)

        @block.tensor
        def _(eng):
            run("pe", eng)

        @block.scalar
        def _(eng):
            run("act", eng)

        @block.vector
        def _(eng):
            run("dve", eng)

        @block.gpsimd
        def _(eng):
            run("pool", eng)

        @block.sync
        def _(eng):
            run("sp", eng)


class T:
    def __init__(self, ap, name, buf=None):
        self.ap = ap
        self.b = buf if buf is not None else Buf(name)

    def __getitem__(self, k):
        return self.ap[k]


SB_LIMIT = 212480


def build_nc(debug=False):
    nc = bass.Bass("TRN2", target_bir_lowering=False)

    def din(name, shape):
        return nc.dram_tensor(name, shape, F32, kind="ExternalInput").ap()

    x = din("x", [NTOK, D])
    w_in = din("w_in", [D, D_IN])
    b_in = din("b_in", [1, D_IN])
    sg_ln_g = din("sg_ln_g", [1, D])
    sg_ln_b = din("sg_ln_b", [1, D])
    sg_w_s = din("sg_w_s", [8 * 128, 128])
    sg_b_s = din("sg_b_s", [1, D])
    wg2_d = din("gla_w_gate2", [16, 512])
    bgate_d = din("gla_b_gate", [1, 512])
    gnorm_d = din("gla_norm_g", [1, D])
    w_out = din("w_out", [D, D])
    ln1_g = din("ln1_g", [1, D])
    ln1_b = din("ln1_b", [1, D])
    w_ff1 = din("w_ff1", [D, D_FF])
    w_ff2 = din("w_ff2", [D_FF, D])
    ln2_g = din("ln2_g", [1, D])
    ln2_b = din("ln2_b", [1, D])
    cmask_d = din("cmask", [128, 8])
    out = nc.dram_tensor("out", [NTOK, D], F32, kind="ExternalOutput").ap()
    dbg = None
    if debug:
        dbg = nc.dram_tensor("dbg", [128, 8 * NTOK], F32, kind="ExternalOutput").ap()
    cc_in = nc.dram_tensor("cc_in", [128, 1028], F32)
    cc_out = nc.dram_tensor("cc_out", [512, 1028], F32)

    P = Prog()
    with ExitStack() as es:
        R = es.enter_context(nc.sbuf_tensor("R", [128, SB_LIMIT // 4], F32))
        banks = [es.enter_context(nc.psum_tensor("bank%d" % i, [128, 512], F32)) for i in range(8)]
        bankb = [Buf("bank%d" % i, excl=True) for i in range(8)]
        esem = {e: es.enter_context(nc.semaphore("s_" + e)) for e in ["pe", "act", "dve", "pool"]}
        dsems = [es.enter_context(nc.semaphore("d%d" % i)) for i in range(N_DMA_SEMS + 1)]
        block = es.enter_context(nc.Block())

        def carve(off, nbytes, dtype, name, shape=None, parts=128, buf=None):
            assert off % 4 == 0 and nbytes % 4 == 0 and off + nbytes <= SB_LIMIT, (name, off, nbytes)
            ap = R[0:parts, off // 4:(off + nbytes) // 4]
            if dtype is BF16:
                ap = ap.bitcast(BF16)
            if shape is not None:
                assert len(shape) == 2
                ap = ap.rearrange("p (a b) -> p a b", a=shape[0])
            return T(ap, name, buf)

        class Bump:
            def __init__(self, start, end):
                self.start, self.end, self.cur = start, end, start

            def alloc(self, nbytes, dtype, name, shape=None, parts=128):
                t = carve(self.cur, nbytes, dtype, name, shape, parts)
                self.cur += (nbytes + 31) // 32 * 32
                assert self.cur <= self.end, (name, self.cur, self.end)
                return t

        OFF_MT = 0
        OFF_CONST = 32768
        OFF_R1 = 36864
        OFF_W = OFF_R1 + 147456
        mT = carve(OFF_MT, 32768, BF16, "mT", (8, NTOK))
        mTb = [Buf("mT%d" % i) for i in range(NT)]
        cst = Bump(OFF_CONST, OFF_R1)
        ident = cst.alloc(512, F32, "ident")
        cols = cst.alloc(80 * 4, F32, "cols")
        c_one = cst.alloc(4, F32, "c_one")
        c_lns = cst.alloc(4, F32, "c_lns")
        c_mh = cst.alloc(16, F32, "c_mh")
        c_zero = cst.alloc(4, F32, "c_zero")
        c_eps = cst.alloc(4, F32, "c_eps")
        cmask = cst.alloc(32, F32, "cmask")
        balow = cst.alloc(4, F32, "balow")
        st6 = cst.alloc(48, F32, "st6")
        st6_hb = [Buf("st6_h%d" % h) for h in range(2)]
        mv = cst.alloc(8, F32, "mv")
        rstd = cst.alloc(16, F32, "rstd")
        ssq = cst.alloc(16, F32, "ssq")
        dec = cst.alloc(16, F32, "dec")
        Dacc = cst.alloc(16, F32, "Dacc")
        dprime = cst.alloc(16, F32, "dprime")
        wal = cst.alloc(8 * 16 * 2, BF16, "wal", (8, 16))

        xT = carve(OFF_R1, 32768, BF16, "xT", (8, NTOK))
        xTb = [Buf("xT%d" % i) for i in range(NT)]
        arena = carve(OFF_R1 + 32768, 65536, BF16, "arena", (8, 4096))
        arb = {(kc, s): Buf("ar%d_%d" % (kc, s)) for kc in range(8) for s in range(4)}
        alT = carve(OFF_R1 + 98304, 8192, F32, "alT", parts=16)
        alTb = [Buf("alT%d" % n) for n in range(4)]
        p1 = Bump(OFF_R1 + 106496, SB_LIMIT)

        def pe_mm(out_ap, bank_i, lhsT, rhs, reads, start=True, stop=True):
            P.op("pe", lambda e: e.matmul(out_ap, lhsT, rhs, start=start, stop=stop), reads, [bankb[bank_i]])

        def pe_tr(out_ap, bank_i, in_ap, reads):
            P.op("pe", lambda e: e.transpose(out_ap, in_ap, ident.ap), list(reads) + [ident.b], [bankb[bank_i]])

        def act(out_ap, in_ap, func, reads, writes, bias=None, scale=1.0, accum=None):
            kw = {}
            if bias is not None:
                kw["bias"] = bias
            if accum is not None:
                kw["accum_out"] = accum
            P.op("act", lambda e: e.activation(out=out_ap, in_=in_ap, func=func, scale=scale, **kw), reads, writes)

        def tt(eng, out_ap, in0, in1, op, reads, writes):
            P.op(eng, lambda e: e.tensor_tensor(out=out_ap, in0=in0, in1=in1, op=op), reads, writes)

        def ts(eng, out_ap, in0, s1, s2, op0, op1, reads, writes):
            if op1 is None:
                P.op(eng, lambda e: e.tensor_scalar(out=out_ap, in0=in0, scalar1=s1, scalar2=None, op0=op0), reads, writes)
            else:
                P.op(eng, lambda e: e.tensor_scalar(out=out_ap, in0=in0, scalar1=s1, scalar2=s2, op0=op0, op1=op1), reads, writes)

        def stt(out_ap, in0, scalar, in1, op0, op1, reads, writes):
            P.op("dve", lambda e: e.scalar_tensor_tensor(out=out_ap, in0=in0, scalar=scalar, in1=in1, op0=op0, op1=op1), reads, writes)

        def cp(eng, out_ap, in_ap, reads, writes):
            if eng == "act":
                P.op("act", lambda e: e.copy(out=out_ap, in_=in_ap), reads, writes)
            else:
                P.op(eng, lambda e: e.tensor_copy(out=out_ap, in_=in_ap), reads, writes)

        def dma(eng, out_ap, in_ap, reads, writes):
            P.dma(eng, lambda e: e.dma_start(out=out_ap, in_=in_ap), reads, writes)

        def memset(eng, t, val):
            P.op(eng, lambda e: e.memset(t.ap, val), [], [t.b])

        def bk(i, a=0, b=512):
            return banks[i][:, a:b]

        def bk3(i, nb=4):
            return banks[i][:, :].rearrange("p (a b) -> p a b", a=nb)

        xs = [carve(SB_LIMIT - 8192 + i * 4096, 4096, F32, "xs%d" % i) for i in range(2)]
        for i in range(2):
            dma("sp", xs[i].ap, x[i * 128:(i + 1) * 128, :], [], [xs[i].b])
        memset("pool", c_one, 1.0)
        memset("pool", c_lns, math.log(Q_SCALE))
        memset("pool", c_mh, -0.5)
        memset("pool", c_zero, 0.0)
        memset("pool", c_eps, LN_EPS)
        memset("pool", ident, 1.0)
        P.op("pool", lambda e: e.affine_select(out=ident.ap, in_=ident.ap, pattern=[[-1, 128]], compare_op=ALU.is_equal,
                                               fill=0.0, base=0, channel_multiplier=1), [ident.b], [ident.b])
        dma("sp", cmask.ap, cmask_d[:, :], [], [cmask.b])
        dma("sp", balow.ap[0:16, :], b_in[0, C_AL:C_AL + 16].rearrange("(p o) -> p o", o=1), [], [balow.b])
        dma("pool", wal.ap, w_in[:, C_AL:C_AL + 16].rearrange("(k p) c -> p k c", p=128), [], [wal.b])

        rows = p1.alloc(512, F32, "rows")
        memset("pool", rows, 0.0)
        dma("sp", rows.ap[0:40, :], b_in[0, 0:C_AL].rearrange("(j p) -> j p", p=128), [], [rows.b])
        dma("sp", rows.ap[40:56, :], b_in[0, C_GA:D_IN].rearrange("(j p) -> j p", p=128), [], [rows.b])
        dma("sp", rows.ap[56:64, :], sg_ln_g[0, :].rearrange("(j p) -> j p", p=128), [], [rows.b])
        dma("sp", rows.ap[64:72, :], sg_ln_b[0, :].rearrange("(j p) -> j p", p=128), [], [rows.b])
        dma("sp", rows.ap[72:80, :], gnorm_d[0, :].rearrange("(j p) -> j p", p=128), [], [rows.b])
        P.wait_all("pe", [rows.b])
        for k in range(N_DMA_SEMS):
            if P.dma_cnt[k]:
                P._wait("pe", ("dma", k, P.dma_cnt[k]))
        pe_tr(bk(0, 0, 128), 0, rows.ap, [rows.b])
        cp("dve", cols.ap, bk(0, 0, 80), [bankb[0]], [cols.b])

        def col(j):
            return cols.ap[:, j:j + 1]

        def load_arena(dst, dstb, src, c0, ncols, a0):
            nkc = dst.ap.shape[1]
            for kc in range(nkc):
                o = 0
                while o < ncols:
                    a = a0 + o
                    n = min(ncols - o, 1024 - (a % 1024))
                    dma("pool", dst.ap[:, kc, a:a + n], src[kc * 128:(kc + 1) * 128, c0 + o:c0 + o + n],
                        [], [dstb[(kc, a // 1024)]])
                    o += n

        def arr(kc, a):
            return arb[(kc, a // 1024)]

        Linc = p1.alloc(512, F32, "Linc")
        Ust = p1.alloc(512, F32, "Ust")
        mask4 = p1.alloc(2048, F32, "mask4")
        wg2 = p1.alloc(2048, F32, "wg2", parts=16)
        bgate = p1.alloc(2048, F32, "bgate")
        bias_bc = p1.alloc(4096, F32, "bias_bc")
        S = p1.alloc(4096, F32, "S", (4, 256))
        S_hb = [Buf("S_h%d" % h) for h in range(4)]
        S_bf = p1.alloc(2048, BF16, "S_bf", (4, 256))
        maskT = p1.alloc(512, F32, "maskT")
        wst8 = p1.alloc(4096, F32, "wst8", (8, 128))
        p1_base = p1.cur

        memset("pool", Linc, -1.0 / GATE_TEMP)
        P.op("pool", lambda e: e.affine_select(out=Linc.ap, in_=Linc.ap, pattern=[[1, 128]], compare_op=ALU.is_ge,
                                               fill=0.0, base=0, channel_multiplier=-1), [Linc.b], [Linc.b])
        memset("pool", Ust, -1.0 / GATE_TEMP)
        P.op("pool", lambda e: e.affine_select(out=Ust.ap, in_=Ust.ap, pattern=[[-1, 128]], compare_op=ALU.is_gt,
                                               fill=0.0, base=0, channel_multiplier=1), [Ust.b], [Ust.b])
        memset("pool", mask4, 1.0)
        P.op("pool", lambda e: e.affine_select(out=mask4.ap, in_=mask4.ap, pattern=[[0, 4], [1, 128]], compare_op=ALU.is_ge,
                                               fill=0.0, base=0, channel_multiplier=-1), [mask4.b], [mask4.b])
        memset("pool", maskT, 1.0)
        P.op("pool", lambda e: e.affine_select(out=maskT.ap, in_=maskT.ap, pattern=[[-1, 128]], compare_op=ALU.is_ge,
                                               fill=0.0, base=0, channel_multiplier=1), [maskT.b], [maskT.b])
        dma("sp", wg2.ap, wg2_d[:, :], [], [wg2.b])
        dma("sp", bgate.ap, bgate_d[0, :].partition_broadcast(128), [], [bgate.b])

        arenaA = T(mT.ap, "arenaA")
        arbA = {(kc, sg_): Buf("arA%d_%d" % (kc, sg_)) for kc in range(8) for sg_ in range(2)}
        load_arena(arenaA, arbA, w_in, C_K, 512, 512)
        load_arena(arenaA, arbA, w_in, C_V, 1024, 1024)
        xT4 = xT.ap

        def alow_group(n):
            for kc in range(8):
                pe_mm(banks[4][0:16, :], 4, wal.ap[:, kc, :], xT4[:, kc, n * 512:(n + 1) * 512],
                      [wal.b] + xTb[n * 4:(n + 1) * 4], start=(kc == 0), stop=(kc == 7))
            act(alT.ap[:, n * 512:(n + 1) * 512], banks[4][0:16, :], AF.Identity, [bankb[4], balow.b], [alTb[n]],
                bias=balow.ap[0:16, :])

        for i in range(NT):
            s = xs[i % 2]
            if i >= 2:
                dma("sp", s.ap, x[i * 128:(i + 1) * 128, :], [], [s.b])
            if i % 4 == 0 and i >= 4:
                alow_group(i // 4 - 1)
            b0 = (i % 2) * 2
            for j in range(8):
                pe_tr(bk(b0 + j // 4, (j % 4) * 128, (j % 4 + 1) * 128), b0 + j // 4, s.ap[:, j * 128:(j + 1) * 128], [s.b])
            cp("act", xT4[:, 0:4, i * 128:(i + 1) * 128], bk3(b0), [bankb[b0]], [xTb[i]])
            cp("dve", xT4[:, 4:8, i * 128:(i + 1) * 128], bk3(b0 + 1), [bankb[b0 + 1]], [xTb[i]])

        gate_b = Buf("p0_done")
        P.op("pool", lambda e: e.memset(c_zero.ap, 0.0), [xTb[NT - 3]], [c_zero.b, gate_b])
        def load_after(dst, dstb, src, c0, ncols, a0):
            nkc = dst.ap.shape[1]
            for kc in range(nkc):
                dma("pool", dst.ap[:, kc, a0:a0 + ncols], src[kc * 128:(kc + 1) * 128, c0:c0 + ncols],
                    [gate_b], [dstb[(kc, a0 // 1024)]])
        wst8g = [Buf("wst8_%d" % g) for g in range(8)]
        for g in range(8):
            dma("sp", wst8.ap[:, g, :], sg_w_s[g * 128:(g + 1) * 128, :], [gate_b], [wst8g[g]])
        load_after(arena, arb, w_in, C_U, 1024, 0)
        load_after(arena, arb, w_in, C_GA, 1024, 2048)
        load_after(arena, arb, w_in, C_VS, 1024, 1024)
        load_after(arena, arb, w_in, C_GB, 1024, 3072)
        alow_group(3)

        dec2 = [cst.alloc(16, F32, "dec%d" % i) for i in range(2)]

        def gla_pass(full, ar, arbufs):
            p1.cur = p1_base
            zb = [p1.alloc(2048, F32, "zb%d" % i) for i in range(2)]
            lb = p1.alloc(2048, F32, "lb")
            Erev = p1.alloc(2048, F32, "Erev")
            kdec = [p1.alloc(1024, BF16, "kdec%d" % i) for i in range(2)]
            vsb = [p1.alloc(2048, BF16, "vsb%d" % i) for i in range(2)]
            kT = p1.alloc(4096, F32, "kT", (4, ST))
            kT_hb = [Buf("kT_h%d" % h) for h in range(4)]
            qT_hb = [Buf("qT_h%d" % h) for h in range(4)]
            temps = zb + [lb, Erev] + kdec + vsb
            if full:
                EkT = p1.alloc(2048, F32, "EkT")
                EqT = p1.alloc(2048, F32, "EqT")
                qpT = [p1.alloc(1024, BF16, "qpT%d" % i, (4, 128)) for i in range(2)]
                kpT = [p1.alloc(1024, BF16, "kpT%d" % i, (4, 128)) for i in range(2)]
                scm = p1.alloc(1024, BF16, "scm")
                on = p1.alloc(4096, F32, "on")
                on_hb = [Buf("on_h%d" % h) for h in range(4)]
                ssq_hb = [Buf("ssq_h%d" % h) for h in range(4)]
                qT = p1.alloc(4096, F32, "qT", (4, ST))
                gbT = p1.alloc(8192, F32, "gbT", (8, ST))
                gbT_vb = [Buf("gbT_v%d" % v) for v in range(8)]
                sgt = [p1.alloc(1024, F32, "sgt%d" % i) for i in range(2)]
                temps += [EkT, EqT, scm] + qpT + kpT + sgt
            P.fence([t.b for t in temps] + (on_hb + ssq_hb + gbT_vb if full else []) + kT_hb + (qT_hb if full else []))
            dma("sp", bias_bc.ap, b_in[0, C_V:C_V + 1024].partition_broadcast(128), [], [bias_bc.b])
            pb = [0]

            def arr_(kc, a):
                return arbufs[(kc, a // 1024)]

            def proj_fm(a0, t0, evac):
                bi = pb[0] % 2
                pb[0] += 1
                for kc in range(8):
                    pe_mm(bk(bi, 0, ST), bi, ar.ap[:, kc, a0:a0 + 128], xT4[:, kc, t0:t0 + ST],
                          [arr_(kc, a0)] + xTb[t0 // 128:t0 // 128 + TPS], start=(kc == 0), stop=(kc == 7))
                evac(bi)

            def proj_a(st):
                t0 = st * ST
                for hb in range(4):
                    proj_fm(512 + hb * 128, t0,
                            lambda bi, hb=hb: ts("dve", kT.ap[:, hb, :], bk(bi, 0, ST), col(bcol(C_K + hb * 128)), None, ALU.add, None,
                                                 [bankb[bi], cols.b], [kT_hb[hb]]))
                if full:
                    for hb in range(4):
                        proj_fm(hb * 128, t0,
                                lambda bi, hb=hb: ts("dve", qT.ap[:, hb, :], bk(bi, 0, ST), col(bcol(C_Q + hb * 128)), None, ALU.add, None,
                                                     [bankb[bi], cols.b], [qT_hb[hb]]))

            def proj_b(st):
                t0 = st * ST
                for vb in range(8):
                    proj_fm(2048 + vb * 128, t0,
                            lambda bi, vb=vb: act(gbT.ap[:, vb, :], bk(bi, 0, ST), AF.Silu, [bankb[bi], cols.b], [gbT_vb[vb]],
                                                  bias=col(bcol(C_R + vb * 128))))
                for vb in range(8):
                    sg = sgt[vb % 2]

                    def ev(bi, vb=vb, sg=sg):
                        act(sg.ap, bk(bi, 0, ST), AF.Sigmoid, [bankb[bi], cols.b], [sg.b], bias=col(bcol(C_GB + vb * 128)))
                        stt(gbT.ap[:, vb, :], gbT.ap[:, vb, :], col(COL_GN + vb), sg.ap, ALU.mult, ALU.mult,
                            [gbT_vb[vb], sg.b, cols.b], [gbT_vb[vb]])
                    proj_fm(3072 + vb * 128, t0, ev)

            def early(i):
                p = i % 2
                j = i % TPS
                tc0, tc1 = j * 128, (j + 1) * 128
                z, dc, kd, vs_ = zb[p], dec2[p], kdec[p], vsb[p]

                def e0():
                    pe_mm(bk(2), 2, alT.ap[:, i * 128:(i + 1) * 128], wg2.ap, [alTb[i // 4], wg2.b])
                    tt("dve", z.ap, bk(2), bgate.ap, ALU.add, [bankb[2], bgate.b], [z.b])
                    act(lb.ap, z.ap, AF.Exp, [z.b], [lb.b], scale=-1.0)
                    act(z.ap, lb.ap, AF.Ln, [lb.b, c_one.b], [z.b], bias=c_one.ap)

                def e1():
                    for h in range(4):
                        pe_mm(bk(3, h * 128, (h + 1) * 128), 3, z.ap[:, h * 128:(h + 1) * 128], Linc.ap, [z.b, Linc.b])
                    pe_mm(bk(4), 4, Ust.ap, z.ap, [Ust.b, z.b])
                    act(dc.ap, bk3(3)[:, :, 127], AF.Exp, [bankb[3]], [dc.b])
                    if full:
                        act(EkT.ap, bk(3), AF.Exp, [bankb[3]], [EkT.b], scale=-1.0)
                        act(EqT.ap, bk(3), AF.Exp, [bankb[3], c_lns.b], [EqT.b], bias=c_lns.ap)
                    act(Erev.ap, bk(4), AF.Exp, [bankb[4]], [Erev.b])
                    if full:
                        tt("dve", kpT[p].ap, kT.ap[:, :, tc0:tc1], EkT.ap.rearrange("p (a b) -> p a b", a=4), ALU.mult,
                           kT_hb + [EkT.b], [kpT[p].b])
                        tt("dve", qpT[p].ap, qT.ap[:, :, tc0:tc1], EqT.ap.rearrange("p (a b) -> p a b", a=4), ALU.mult,
                           qT_hb + [EqT.b], [qpT[p].b])

                def e2():
                    for h in range(4):
                        pe_tr(bk(5, h * 128, (h + 1) * 128), 5, kT.ap[:, h, tc0:tc1], [kT_hb[h]])
                    tt("dve", kd.ap, bk(5), Erev.ap, ALU.mult, [bankb[5], Erev.b], [kd.b])

                def e3(halves=(0, 1)):
                    for half in halves:
                        for kc in range(8):
                            pe_mm(bk(6 + half), 6 + half, xT4[:, kc, i * 128:(i + 1) * 128],
                                  ar.ap[:, kc, 1024 + half * 512:1024 + (half + 1) * 512],
                                  [xTb[i], arr_(kc, 1024)], start=(kc == 0), stop=(kc == 7))
                        tt("dve", vs_.ap[:, half * 512:(half + 1) * 512], bk(6 + half), bias_bc.ap[:, half * 512:(half + 1) * 512],
                           ALU.add, [bankb[6 + half], bias_bc.b], [vs_.b])
                return [e0, e1, e2, e3]

            def late(i):
                p = i % 2
                j = i % TPS
                tc0, tc1 = j * 128, (j + 1) * 128
                dc, kd, vs_ = dec2[p], kdec[p], vsb[p]
                OB = (1, 2)
                TB = (3, 4)
                UB = (5, 0)

                def l0():
                    if not full:
                        return
                    for h in range(4):
                        pe_mm(bk(0, h * 128, (h + 1) * 128), 0, kpT[p].ap[:, h, :], qpT[p].ap[:, h, :], [kpT[p].b, qpT[p].b])
                    tt("dve", scm.ap, bk(0), mask4.ap, ALU.mult, [bankb[0], mask4.b], [scm.b])

                def l1():
                    if not full:
                        return
                    for h in range(4):
                        ob, oc = OB[h // 2], (h % 2) * 256
                        pe_mm(bk(ob, oc, oc + 256), ob, scm.ap[:, h * 128:(h + 1) * 128], vs_.ap[:, h * 256:(h + 1) * 256],
                              [scm.b, vs_.b], start=True, stop=False)
                        pe_mm(bk(ob, oc, oc + 256), ob, qpT[p].ap[:, h, :], S_bf.ap[:, h, :], [qpT[p].b, S_bf.b],
                              start=False, stop=True)
                    for h in range(4):
                        ob, oc = OB[h // 2], (h % 2) * 256
                        act(on.ap[:, h * 256:(h + 1) * 256], bk(ob, oc, oc + 256), AF.Square, [bankb[ob]], [on_hb[h], ssq_hb[h]],
                            accum=ssq.ap[:, h:h + 1])
                    act(rstd.ap, ssq.ap, AF.Ln, ssq_hb + [c_eps.b], [rstd.b], bias=c_eps.ap, scale=1.0 / 256.0)
                    act(rstd.ap, rstd.ap, AF.Exp, [rstd.b], [rstd.b], scale=-0.5)
                    for h in range(4):
                        ob, oc = OB[h // 2], (h % 2) * 256
                        act(on.ap[:, h * 256:(h + 1) * 256], bk(ob, oc, oc + 256), AF.Identity, [bankb[ob], rstd.b], [on_hb[h]],
                            scale=rstd.ap[:, h:h + 1])

                def l2():
                    if not full:
                        return
                    for vb in range(8):
                        tb = TB[vb // 4]
                        pe_tr(bk(tb, (vb % 4) * 128, (vb % 4 + 1) * 128), tb, on.ap[:, vb * 128:(vb + 1) * 128], [on_hb[vb // 2]])
                    on3 = on.ap.rearrange("p (a b) -> p a b", a=8)
                    for hf in range(2):
                        tb = TB[hf]
                        tt("dve", on3[:, hf * 4:(hf + 1) * 4, :], bk3(tb), gbT.ap[:, hf * 4:(hf + 1) * 4, tc0:tc1], ALU.mult,
                           [bankb[tb]] + gbT_vb[hf * 4:(hf + 1) * 4], [on_hb[2 * hf], on_hb[2 * hf + 1]])
                    msl = mT.ap[:, :, i * 128:(i + 1) * 128]
                    tt("dve", msl, on3, msl, ALU.add, on_hb + [mTb[i]], [mTb[i]])

                def l3():
                    for h in range(4):
                        ub, uc = UB[h // 2], (h % 2) * 256
                        pe_mm(bk(ub, uc, uc + 256), ub, kd.ap[:, h * 128:(h + 1) * 128], vs_.ap[:, h * 256:(h + 1) * 256],
                              [kd.b, vs_.b])
                    for h in range(4):
                        ub, uc = UB[h // 2], (h % 2) * 256
                        stt(S.ap[:, h, :], S.ap[:, h, :], dc.ap[:, h:h + 1], bk(ub, uc, uc + 256), ALU.mult, ALU.add,
                            [S_hb[h], dc.b, bankb[ub]], [S_hb[h]])
                    if full:
                        cp("dve", S_bf.ap, S.ap, S_hb, [S_bf.b])
                    else:
                        tt("dve", Dacc.ap, Dacc.ap, dc.ap, ALU.mult, [Dacc.b, dc.b], [Dacc.b])
                return [l0, l1, l2, l3]

            for i in range(NT + 1):
                if i < NT and i % TPS == 0:
                    proj_a(i // TPS)
                E = early(i) if i < NT else [lambda *a: None] * 4
                L = late(i - 1) if i >= 1 else [lambda *a: None] * 4
                E[0]()
                L[0]()
                E[3]((0,))
                E[1]()
                L[1]()
                E[3]((1,))
                E[2]()
                L[3]()
                L[2]()
                if full and i < NT and i % TPS == 0:
                    proj_b(i // TPS)

        P.op("dve", lambda e: e.memset(S.ap, 0.0), [], S_hb)
        memset("dve", Dacc, 1.0)
        gla_pass(False, arenaA, arbA)
        p1.cur = p1_base
        pay = p1.alloc(4128, F32, "pay")
        P.fence([pay.b])
        cp("dve", pay.ap[:, 0:1024], S.ap.rearrange("p a b -> p (a b)"), S_hb, [pay.b])
        cp("dve", pay.ap[:, 1024:1028], Dacc.ap, [Dacc.b, pay.b], [pay.b])
        bcc_in, bcc_out = Buf("cc_in"), Buf("cc_out")
        dma("sp", cc_in.ap()[:, :], pay.ap[:, 0:1028], [pay.b], [bcc_in])
        cc_tok = P.ext("pool", lambda e: e.collective_compute("AllGather", ALU.bypass, replica_groups=[[0, 1, 2, 3], [4, 5, 6, 7]],
                                                              ins=[cc_in.ap().opt()], outs=[cc_out.ap().opt()]),
                       [bcc_in], [bcc_out])
        P._wait("pool", cc_tok)
        p1.cur = p1_base
        wsT = p1.alloc(2048, BF16, "wsT", (8, 128))
        Cg = p1.alloc(4096, F32, "Cg", (8, 128))
        wst = p1.alloc(512, F32, "wst")
        wsf = p1.alloc(512, F32, "wsf")
        onesf = p1.alloc(512, F32, "onesf")
        ugT2 = [p1.alloc(8192, F32, "ugT%d" % i, (8, ST)) for i in range(2)]
        ugT_cb = [[Buf("ugT%d_c%d" % (i, c)) for c in range(8)] for i in range(2)]
        sgb1 = [p1.alloc(1024, F32, "sgb1_%d" % i) for i in range(2)]
        zt2 = [p1.alloc(4096, F32, "zt%d" % i) for i in range(2)]
        vn2 = [p1.alloc(2048, BF16, "vn%d" % i) for i in range(2)]
        yat2 = [p1.alloc(4096, F32, "yat%d" % i, (8, 128)) for i in range(2)]
        yat_gb = [[Buf("yat%d_g%d" % (i, g)) for g in range(8)] for i in range(2)]
        P.fence([wsT.b, Cg.b, wst.b, wsf.b, onesf.b, sgb1[0].b, sgb1[1].b]
                + [t.b for t in zt2 + vn2 + yat2] + [b_ for l_ in ugT_cb for b_ in l_])
        P.fence(mTb)
        dma("sp", bias_bc.ap, b_in[0, C_VS:C_VS + 1024].partition_broadcast(128), [], [bias_bc.b])
        dma("sp", Cg.ap.rearrange("p a b -> p (a b)"), sg_b_s[0, :].partition_broadcast(128), [], [Cg.b])
        memset("dve", onesf, 1.0)
        wsf8 = T(yat2[1].ap, "wsf8")
        P.fence([wsf8.b])
        for g in range(8):
            tt("dve", wst8.ap[:, g, :], wst8.ap[:, g, :], maskT.ap, ALU.mult, [wst8.b, wst8g[g], maskT.b], [wst8.b])
        for g in range(8):
            tb = 2 + g // 4
            pe_tr(bk(tb, (g % 4) * 128, (g % 4 + 1) * 128), tb, wst8.ap[:, g, :], [wst8.b])
        for hf in range(2):
            cp("act", wsf8.ap[:, hf * 4:(hf + 1) * 4, :], bk3(2 + hf), [bankb[2 + hf]], [wsf8.b])
            cp("dve", wsT.ap[:, hf * 4:(hf + 1) * 4, :], bk3(2 + hf), [bankb[2 + hf]], [wsT.b])
        for g in range(8):
            tb = 4 + g // 4
            pe_mm(bk(tb, (g % 4) * 128, (g % 4 + 1) * 128), tb, onesf.ap, wsf8.ap[:, g, :], [onesf.b, wsf8.b])
        for g in range(8):
            tb = 4 + g // 4
            stt(Cg.ap[:, g, :], bk(tb, (g % 4) * 128, (g % 4 + 1) * 128), col(COL_SGB + g), Cg.ap[:, g, :], ALU.mult, ALU.add,
                [bankb[tb], cols.b, Cg.b], [Cg.b])
        P.fence([b_ for l_ in yat_gb for b_ in l_])
        pb1 = [0]

        def b1_stproj(st, part):
            t0 = st * ST
            ugT = ugT2[st % 2]
            for cb in (range(8) if part == 0 else []):
                bi = pb1[0] % 2
                pb1[0] += 1
                for kc in range(8):
                    pe_mm(bk(bi, 0, ST), bi, arena.ap[:, kc, cb * 128:(cb + 1) * 128], xT4[:, kc, t0:t0 + ST],
                          [arr(kc, cb * 128)] + xTb[t0 // 128:t0 // 128 + TPS], start=(kc == 0), stop=(kc == 7))
                act(ugT.ap[:, cb, :], bk(bi, 0, ST), AF.Gelu_apprx_tanh, [bankb[bi], cols.b], [ugT_cb[st % 2][cb]], bias=col(bcol(C_U + cb * 128)))
            for cb in (range(8) if part == 1 else []):
                bi = pb1[0] % 2
                pb1[0] += 1
                for kc in range(8):
                    pe_mm(bk(bi, 0, ST), bi, arena.ap[:, kc, 2048 + cb * 128:2048 + (cb + 1) * 128], xT4[:, kc, t0:t0 + ST],
                          [arr(kc, 2048)] + xTb[t0 // 128:t0 // 128 + TPS], start=(kc == 0), stop=(kc == 7))
                sg = sgb1[cb % 2]
                act(sg.ap, bk(bi, 0, ST), AF.Sigmoid, [bankb[bi], cols.b], [sg.b], bias=col(bcol(C_GA + cb * 128)))
                tt("dve", ugT.ap[:, cb, :], ugT.ap[:, cb, :], sg.ap, ALU.mult, [ugT_cb[st % 2][cb], sg.b], [ugT_cb[st % 2][cb]])

        def b1_vs(i):
            zt, vn = zt2[i % 2], vn2[i % 2]
            vb0 = 2 if i % 2 == 0 else 6
            for half in range(2):
                for kc in range(8):
                    pe_mm(bk(vb0 + half), vb0 + half, xT4[:, kc, i * 128:(i + 1) * 128],
                          arena.ap[:, kc, 1024 + half * 512:1024 + (half + 1) * 512],
                          [xTb[i], arr(kc, 1024)], start=(kc == 0), stop=(kc == 7))
                tt("dve", zt.ap[:, half * 512:(half + 1) * 512], bk(vb0 + half), bias_bc.ap[:, half * 512:(half + 1) * 512],
                   ALU.add, [bankb[vb0 + half], bias_bc.b], [zt.b])
            act(zt.ap, zt.ap, AF.Gelu_apprx_tanh, [zt.b], [zt.b])
            for half in range(2):
                P.op("dve", lambda e, half=half: e.bn_stats(out=st6.ap[:, half * 6:(half + 1) * 6],
                                                           in_=zt.ap[:, half * 512:(half + 1) * 512]), [zt.b], [st6_hb[half]])
            P.op("dve", lambda e: e.bn_aggr(out=mv.ap, in_=st6.ap), st6_hb, [mv.b])
            if i >= NT - 3:
                act(rstd.ap[:, 0:1], mv.ap[:, 1:2], AF.Ln, [mv.b, c_eps.b], [rstd.b], bias=c_eps.ap)
                act(rstd.ap[:, 0:1], rstd.ap[:, 0:1], AF.Exp, [rstd.b], [rstd.b], scale=-0.5)
            else:
                ts("dve", rstd.ap[:, 0:1], mv.ap[:, 1:2], LN_EPS, None, ALU.add, None, [mv.b], [rstd.b])
                tt("pool", rstd.ap[:, 0:1], rstd.ap[:, 0:1], c_mh.ap[:, 0:1], ALU.pow, [rstd.b, c_mh.b], [rstd.b])
            ts("dve", vn.ap, zt.ap, mv.ap[:, 0:1], rstd.ap[:, 0:1], ALU.subtract, ALU.mult, [zt.b, mv.b, rstd.b], [vn.b])

        def b1_mix(i):
            vn, yat, ugT = vn2[i % 2], yat2[i % 2], ugT2[(i // TPS) % 2]
            j = i % TPS
            for g in range(8):
                mb = 4 + g // 4
                pe_mm(bk(mb, (g % 4) * 128, (g % 4 + 1) * 128), mb, vn.ap[:, g * 128:(g + 1) * 128], wsT.ap[:, g, :],
                      [vn.b, wsT.b])
            for g in range(8):
                mb = 4 + g // 4
                stt(yat.ap[:, g, :], bk(mb, (g % 4) * 128, (g % 4 + 1) * 128), col(COL_SGG + g), Cg.ap[:, g, :],
                    ALU.mult, ALU.add, [bankb[mb], cols.b, Cg.b], [yat_gb[i % 2][g]])
            tt("dve", mT.ap[:, :, i * 128:(i + 1) * 128], yat.ap, ugT.ap[:, :, j * 128:(j + 1) * 128], ALU.mult,
               yat_gb[i % 2] + ugT_cb[(i // TPS) % 2], [mTb[i]])

        b1_stproj(0, 0)
        b1_stproj(0, 1)
        b1_vs(0)
        for i in range(NT):
            if i // TPS + 1 < NST:
                b1_stproj(i // TPS + 1, i % TPS)
            b1_mix(i)
            if i + 1 < NT:
                b1_vs(i + 1)
            if i == NT - 3:
                load_arena(arena, arb, w_in, C_K, 512, 512)
                load_arena(arena, arb, w_in, C_Q, 512, 0)
                load_arena(arena, arb, w_in, C_R, 1024, 2048)
            if i == NT - 2:
                load_arena(arena, arb, w_in, C_V, 1024, 1024)

        p1.cur = p1_base
        gst3 = [p1.alloc(4128, F32, "gst%d" % j) for j in range(3)]
        mS = p1.alloc(4096, F32, "mS", (4, 256))
        P.fence([t.b for t in gst3] + [mS.b])
        for jq in range(3):
            dma("sp", gst3[jq].ap[:, 0:1028], cc_out.ap()[jq * 128:(jq + 1) * 128, :], [bcc_out], [gst3[jq].b])
        P.op("dve", lambda e: e.memset(S.ap, 0.0), [], S_hb)
        for jq in range(3):
            gst = gst3[jq]
            ts("dve", dprime.ap, gst.ap[:, 1024:1028], cmask.ap[:, jq:jq + 1], cmask.ap[:, 4 + jq:5 + jq], ALU.mult, ALU.add,
               [gst.b, cmask.b], [dprime.b])
            ts("dve", mS.ap.rearrange("p a b -> p (a b)"), gst.ap[:, 0:1024], cmask.ap[:, jq:jq + 1], None, ALU.mult, None,
               [gst.b, cmask.b], [mS.b])
            for h in range(4):
                stt(S.ap[:, h, :], S.ap[:, h, :], dprime.ap[:, h:h + 1], mS.ap[:, h, :], ALU.mult, ALU.add,
                    [S_hb[h], dprime.b, mS.b], [S_hb[h]])
        cp("dve", S_bf.ap, S.ap, S_hb, [S_bf.b])

        p1.cur = p1_base
        gla_pass(True, arena, arb)

        if debug:
            dbgb = Buf("dbg")
            for kc in range(8):
                dma("pool", dbg[:, kc * NTOK:(kc + 1) * NTOK], mT.ap[:, kc, :], mTb, [dbgb])

        W1 = carve(OFF_R1, 65536, BF16, "W1", (8, 4096))
        W2 = carve(OFF_R1 + 65536, 65536, BF16, "W2", (32, 1024))
        Wo = carve(OFF_R1 + 131072, 16384, BF16, "Wo", (8, 1024))
        w1b = {(kc, s): Buf("w1_%d_%d" % (kc, s)) for kc in range(8) for s in range(4)}
        w2b = {(kc, 0): Buf("w2_%d" % kc) for kc in range(32)}
        wob = {(kc, 0): Buf("wo_%d" % kc) for kc in range(8)}
        p2 = Bump(OFF_W, SB_LIMIT)
        xh = [p2.alloc(4096, F32, "xh%d" % i) for i in range(4)]
        rt = [p2.alloc(1024, F32, "rt%d" % i) for i in range(3)]
        lnA = p2.alloc(4096, F32, "lnA")
        lnB = p2.alloc(4096, F32, "lnB")
        aT = [cst.alloc(512, BF16, "aT%d" % i) for i in range(3)]
        P.fence(list(w1b.values()) + list(w2b.values()) + list(wob.values()) + [t.b for t in xh] + [lnA.b, lnB.b]
                + [t.b for t in rt])
        for i in range(4):
            dma("sp", xh[i].ap, x[i * 128:(i + 1) * 128, :], [], [xh[i].b])
        ln_cur = [ln1_g]
        dma("sp", lnA.ap, ln1_g[0, :].partition_broadcast(128), [], [lnA.b])
        dma("sp", lnB.ap, ln1_b[0, :].partition_broadcast(128), [], [lnB.b])
        xdep = [t.b for t in xh] + [lnA.b, lnB.b]
        for kc in range(8):
            dma("pool", Wo.ap[:, kc, :], w_out[kc * 128:(kc + 1) * 128, :], xdep, [wob[(kc, 0)]])
        for seg in range(4):
            for kc in range(8):
                dma("pool", W1.ap[:, kc, seg * 1024:(seg + 1) * 1024], w_ff1[kc * 128:(kc + 1) * 128, seg * 1024:(seg + 1) * 1024],
                    [], [w1b[(kc, seg)]])
            for kc in range(seg * 8, (seg + 1) * 8):
                dma("pool", W2.ap[:, kc, :], w_ff2[kc * 128:(kc + 1) * 128, :], [], [w2b[(kc, 0)]])
        outb = Buf("out")

        def ln_load(g_d, b_d):
            if ln_cur[0] is not g_d:
                dma("sp", lnA.ap, g_d[0, :].partition_broadcast(128), [], [lnA.b])
                dma("sp", lnB.ap, b_d[0, :].partition_broadcast(128), [], [lnB.b])
                ln_cur[0] = g_d

        def layer_norm(xt, g_d, b_d):
            for half in range(2):
                P.op("dve", lambda e, half=half: e.bn_stats(out=st6.ap[:, half * 6:(half + 1) * 6],
                                                           in_=xt.ap[:, half * 512:(half + 1) * 512]), [xt.b], [st6_hb[half]])
            P.op("dve", lambda e: e.bn_aggr(out=mv.ap, in_=st6.ap), st6_hb, [mv.b])
            act(rstd.ap[:, 0:1], mv.ap[:, 1:2], AF.Ln, [mv.b, c_eps.b], [rstd.b], bias=c_eps.ap)
            act(rstd.ap[:, 0:1], rstd.ap[:, 0:1], AF.Exp, [rstd.b], [rstd.b], scale=-0.5)
            ts("dve", xt.ap, xt.ap, mv.ap[:, 0:1], rstd.ap[:, 0:1], ALU.subtract, ALU.mult, [xt.b, mv.b, rstd.b], [xt.b])
            ln_load(g_d, b_d)
            tt("dve", xt.ap, xt.ap, lnA.ap, ALU.mult, [xt.b, lnA.b], [xt.b])
            tt("dve", xt.ap, xt.ap, lnB.ap, ALU.add, [xt.b, lnB.b], [xt.b])

        def pre_chunks(st):
            chunks = []
            for j in range(TPS):
                i = st * TPS + j
                xt = xh[i % 4]

                def c_op(half, i=i, xt=xt):
                    if half == 0 and i >= 4:
                        dma("sp", xt.ap, x[i * 128:(i + 1) * 128, :], [], [xt.b])
                    for kc in range(8):
                        pe_mm(bk(7), 7, mT.ap[:, kc, i * 128:(i + 1) * 128], Wo.ap[:, kc, half * 512:(half + 1) * 512],
                              [mTb[i], wob[(kc, 0)]], start=(kc == 0), stop=(kc == 7))
                    sl = xt.ap[:, half * 512:(half + 1) * 512]
                    stt(sl, sl, ALPHA, bk(7), ALU.mult, ALU.add, [xt.b, bankb[7]], [xt.b])
                    if half == 1:
                        layer_norm(xt, ln1_g, ln1_b)

                def c_tr(hf, i=i, xt=xt):
                    for db in range(hf * 4, (hf + 1) * 4):
                        pe_tr(bk(7, (db % 4) * 128, (db % 4 + 1) * 128), 7, xt.ap[:, db * 128:(db + 1) * 128], [xt.b])
                    cp("act", mT.ap[:, hf * 4:(hf + 1) * 4, i * 128:(i + 1) * 128], bk3(7), [bankb[7]], [mTb[i]])

                chunks.append((j * 10 + 8, lambda f=c_op: f(0)))
                chunks.append((j * 10 + 9, lambda f=c_op: f(1)))
                chunks.append((j * 10 + 16, lambda f=c_tr: f(0)))
                chunks.append((j * 10 + 17, lambda f=c_tr: f(1)))
            return chunks

        def finish_tile(i):
            xt = xh[i % 4]
            layer_norm(xt, ln2_g, ln2_b)
            dma("sp", out[i * 128:(i + 1) * 128, :], xt.ap, [xt.b], [outb])

        for _, f in pre_chunks(0):
            f()
        for st in range(NST):
            sched = {}
            if st >= 1:
                sched[2] = [lambda i=(st - 1) * TPS: finish_tile(i)]
                sched[4] = [lambda i=(st - 1) * TPS + 1: finish_tile(i)]
            if st + 1 < NST:
                sched[5] = [lambda: ln_load(ln1_g, ln1_b)]
                for fbk_, f in pre_chunks(st + 1):
                    sched.setdefault(fbk_, []).append(f)
            sched.setdefault(28, []).append(lambda: ln_load(ln2_g, ln2_b))
            hsl = [mTb[st * TPS + j] for j in range(TPS)]

            def ff1(fb):
                fbk = 4 + fb % 3
                for kc in range(8):
                    pe_mm(bk(fbk, 0, ST), fbk, W1.ap[:, kc, fb * 128:(fb + 1) * 128], mT.ap[:, kc, st * ST:(st + 1) * ST],
                          [w1b[(kc, fb // 8)]] + hsl, start=(kc == 0), stop=(kc == 7))
                r_ = rt[fb % 3]
                a_ = aT[fb % 3]
                act(r_.ap, bk(fbk, 0, ST), AF.Relu, [bankb[fbk]], [r_.b])
                tt("dve", a_.ap, r_.ap, r_.ap, ALU.mult, [r_.b], [a_.b])

            def ff2(fb):
                a_ = aT[fb % 3]
                for j in range(TPS):
                    for half in range(2):
                        ab = j * 2 + half
                        pe_mm(bk(ab), ab, a_.ap[:, j * 128:(j + 1) * 128], W2.ap[:, fb, half * 512:(half + 1) * 512],
                              [a_.b, w2b[(fb, 0)]], start=(fb == 0), stop=(fb == 31))

            ff1(0)
            ff1(1)
            for fb in range(32):
                if fb + 2 < 32:
                    ff1(fb + 2)
                ff2(fb)
                for f in sched.get(fb, []):
                    f()
            for j in range(TPS):
                i = st * TPS + j
                xt = xh[i % 4]
                for half in range(2):
                    sl = xt.ap[:, half * 512:(half + 1) * 512]
                    stt(sl, sl, ALPHA, bk(j * 2 + half), ALU.mult, ALU.add, [xt.b, bankb[j * 2 + half]], [xt.b])
        for j in range(TPS):
            finish_tile((NST - 1) * TPS + j)

        for k in range(N_DMA_SEMS):
            if P.dma_cnt[k]:
                P._wait("sp", ("dma", k, P.dma_cnt[k]))
        P.emit(block, esem, dsems)
    return nc


_NC_CACHE = {}


def _prep_inputs(inputs):
    f = lambda a: np.ascontiguousarray(np.asarray(a, dtype=np.float32))
    x = f(inputs["x"]).reshape(2 * 8192, D)
    shared = {
        "w_in": f(inputs["w_in"]).reshape(D, D_IN),
        "b_in": f(inputs["b_in"]).reshape(1, D_IN),
        "sg_ln_g": f(inputs["sg_ln_g"]).reshape(1, D),
        "sg_ln_b": f(inputs["sg_ln_b"]).reshape(1, D),
        "sg_w_s": f(inputs["sg_w_s"]).reshape(8 * 128, 128),
        "sg_b_s": f(inputs["sg_b_s"]).reshape(1, D),
        "gla_w_gate2": f(inputs["gla_w_gate2"]).reshape(16, 512),
        "gla_b_gate": f(inputs["gla_b_gate"]).reshape(1, 512),
        "gla_norm_g": f(inputs["gla_norm_g"]).reshape(1, D),
        "w_out": f(inputs["w_out"]).reshape(D, D),
        "ln1_g": f(inputs["ln1_g"]).reshape(1, D),
        "ln1_b": f(inputs["ln1_b"]).reshape(1, D),
        "w_ff1": f(inputs["w_ff1"]).reshape(D, D_FF),
        "w_ff2": f(inputs["w_ff2"]).reshape(D_FF, D),
        "ln2_g": f(inputs["ln2_g"]).reshape(1, D),
        "ln2_b": f(inputs["ln2_b"]).reshape(1, D),
    }
    in_maps = []
    for c in range(8):
        qc = c % 4
        cm = np.zeros((128, 8), np.float32)
        for j in range(4):
            cm[:, j] = 1.0 if j < qc else 0.0
            cm[:, 4 + j] = 0.0 if j < qc else 1.0
        m = dict(shared)
        m["x"] = np.ascontiguousarray(x[c * NTOK:(c + 1) * NTOK])
        m["cmask"] = cm
        in_maps.append(m)
    return in_maps


def kernel(**inputs):
    if "nc" not in _NC_CACHE:
        _NC_CACHE["nc"] = build_nc()
    nc = _NC_CACHE["nc"]
    in_maps = _prep_inputs(inputs)
    res = run_bass_kernel_spmd(nc, in_maps, core_ids=list(range(8)))
    outs = [np.asarray(res.results[c]["out"], dtype=np.float32) for c in range(8)]
    return np.concatenate(outs, axis=0).reshape(2, 8192, D)
```
